# Optimizing a Trainium2 kernel written in Bass

```python
import math
import jax, jax.numpy as jnp
from jax import lax
import numpy as np

D_MODEL = 1024
BATCH = 8
SEQ = 4096
DEPTH = 4

GRID_W = 64
CTX_LEN = 256
N_MIXERS = 3
HEAD_DIM = 64
ROPE_BASE = 10000.0
ROPE_PAIRS_PER_AXIS = HEAD_DIM // 4
Q_BLOCK = 128
NORM_EPS = 1e-6
NEG_INF = -1e30
DIFF_HEADS = D_MODEL // (2 * HEAD_DIM)
NA_HEADS = D_MODEL // HEAD_DIM
NA_KH_MAX = 8
NA_KW = 16
SWA_HEADS = D_MODEL // HEAD_DIM
SWA_KV_HEADS = SWA_HEADS // 4
SWA_WINDOW = 128
D_FF = 2816
CONV_W = 3
N_LAYERS_A = (DEPTH + N_MIXERS - 1) // N_MIXERS
N_LAYERS_B = (DEPTH + N_MIXERS - 2) // N_MIXERS
N_LAYERS_C = (DEPTH + N_MIXERS - 3) // N_MIXERS

kernel_name = "hybrid_diff_na_swa_convffn_prefix_dit"


def rmsnorm(t, g):
    tf = t.astype(jnp.float32)
    tf = tf * lax.rsqrt(jnp.mean(tf * tf, axis=-1, keepdims=True) + NORM_EPS)
    return tf.astype(t.dtype) * g


def rope_tables(n_tokens, dtype):
    t = jnp.arange(n_tokens, dtype=jnp.int32)
    row = (t // GRID_W).astype(jnp.float32)
    col = (t % GRID_W).astype(jnp.float32)
    inv = ROPE_BASE ** (-jnp.arange(ROPE_PAIRS_PER_AXIS, dtype=jnp.float32) / ROPE_PAIRS_PER_AXIS)
    ar = row[:, None] * inv[None, :]
    ac = col[:, None] * inv[None, :]
    return tuple(a.astype(dtype) for a in (jnp.cos(ar), jnp.sin(ar), jnp.cos(ac), jnp.sin(ac)))


def _rotate(t, cos, sin):
    t1, t2 = jnp.split(t, 2, axis=-1)
    return jnp.concatenate([t1 * cos - t2 * sin, t2 * cos + t1 * sin], axis=-1)


def rope_2d(t, tables):
    cr, sr, cc, sc = (a[None, :, None, :] for a in tables)
    half = HEAD_DIM // 2
    return jnp.concatenate([_rotate(t[..., :half], cr, sr), _rotate(t[..., half:], cc, sc)], axis=-1)


def _to_blocks(t, blk):
    b, s = t.shape[:2]
    return jnp.moveaxis(t.reshape(b, s // blk, blk, *t.shape[2:]), 1, 0)


def _from_blocks(t):
    t = jnp.moveaxis(t, 0, 1)
    return t.reshape(t.shape[0], t.shape[1] * t.shape[2], *t.shape[3:])


def _diff_attend(q, k, v, lam, subln, lam_init):
    s = jnp.einsum('bqhjd,bkhjd->bhjqk', q, k).astype(jnp.float32) * (HEAD_DIM ** -0.5)
    p = jax.nn.softmax(s, axis=-1)
    a = (p[:, :, 0] - lam * p[:, :, 1]).astype(v.dtype)
    o = jnp.einsum('bhqk,bkhe->bqhe', a, v)
    o = rmsnorm(o, subln) * (1.0 - lam_init)
    return o.reshape(o.shape[0], o.shape[1], -1)


def diff_attention(h, hc, wqkv, wo, lam_vecs, subln, layer_idx, tables, need_ctx):
    B, S, _ = h.shape
    H, dh = DIFF_HEADS, HEAD_DIM
    lam_init = 0.8 - 0.6 * math.exp(-0.3 * layer_idx)
    lv = lam_vecs.astype(jnp.float32)
    lam = jnp.exp(jnp.sum(lv[0] * lv[1])) - jnp.exp(jnp.sum(lv[2] * lv[3])) + lam_init

    def proj(t):
        n = t.shape[1]
        q, k, v = jnp.split(t @ wqkv, 3, axis=-1)
        return q.reshape(B, n, 2 * H, dh), k.reshape(B, n, 2 * H, dh), v.reshape(B, n, H, 2 * dh)

    q, k, v = proj(h)
    q, k = rope_2d(q, tables), rope_2d(k, tables)
    qc, kc, vc = proj(hc)
    q = q.reshape(B, S, H, 2, dh)
    k = k.reshape(B, S, H, 2, dh)
    n_ctx = hc.shape[1]
    qc = qc.reshape(B, n_ctx, H, 2, dh)
    kc = kc.reshape(B, n_ctx, H, 2, dh)
    k_all = jnp.concatenate([k, kc], axis=1)
    v_all = jnp.concatenate([v, vc], axis=1)
    y = _from_blocks(lax.map(lambda qb: _diff_attend(qb, k_all, v_all, lam, subln, lam_init),
                             _to_blocks(q, Q_BLOCK))) @ wo
    yc = (_diff_attend(qc, kc, vc, lam, subln, lam_init) @ wo) if need_ctx else None
    return y, yc


def _mha(q, k, v, mask=None, bias=None):
    s = jnp.einsum('bqhd,bkhd->bhqk', q, k).astype(jnp.float32) * (q.shape[-1] ** -0.5)
    if bias is not None:
        s = s + bias
    if mask is not None:
        s = jnp.where(mask, s, NEG_INF)
    p = jax.nn.softmax(s, axis=-1).astype(v.dtype)
    return jnp.einsum('bhqk,bkhd->bqhd', p, v)


def neighbourhood_attention(h, hc, wqkv, wo, rpb, need_ctx):
    B, S, _ = h.shape
    H, dh = NA_HEADS, HEAD_DIM
    rows = S // GRID_W
    kh = min(NA_KH_MAX, rows)
    kw = NA_KW

    def proj(t):
        n = t.shape[1]
        q, k, v = jnp.split(t @ wqkv, 3, axis=-1)
        return q.reshape(B, n, H, dh), k.reshape(B, n, H, dh), v.reshape(B, n, H, dh)

    q, k, v = proj(h)
    qc, kc, vc = proj(hc)
    n_ctx = hc.shape[1]
    kg = k.reshape(B, rows, GRID_W, H, dh)
    vg = v.reshape(B, rows, GRID_W, H, dh)
    col = jnp.arange(GRID_W, dtype=jnp.int32)
    col_start = jnp.clip(col - kw // 2, 0, GRID_W - kw)
    col_ok = (col[None, :] >= col_start[:, None]) & (col[None, :] < col_start[:, None] + kw)
    dx_idx = jnp.clip(col[None, :] - col[:, None], -(kw - 1), kw - 1) + (kw - 1)
    mask = jnp.concatenate(
        [jnp.broadcast_to(col_ok[:, None, :], (GRID_W, kh, GRID_W)).reshape(GRID_W, kh * GRID_W),
         jnp.ones((GRID_W, n_ctx), dtype=bool)], axis=-1)
    rpb32 = rpb.astype(jnp.float32)
    ctx_bias = jnp.zeros((H, GRID_W, n_ctx), jnp.float32)

    def row_block(args):
        r, q_row = args
        rs = jnp.clip(r - kh // 2, 0, rows - kh)
        k_win = lax.dynamic_slice_in_dim(kg, rs, kh, axis=1).reshape(B, kh * GRID_W, H, dh)
        v_win = lax.dynamic_slice_in_dim(vg, rs, kh, axis=1).reshape(B, kh * GRID_W, H, dh)
        dy_idx = rs + jnp.arange(kh, dtype=jnp.int32) - r + (NA_KH_MAX - 1)
        bias = rpb32[:, dy_idx[None, :, None], dx_idx[:, None, :]].reshape(H, GRID_W, kh * GRID_W)
        bias = jnp.concatenate([bias, ctx_bias], axis=-1)
        o = _mha(q_row, jnp.concatenate([k_win, kc], axis=1), jnp.concatenate([v_win, vc], axis=1),
                 mask, bias)
        return o.reshape(B, GRID_W, H * dh)

    q_rows = jnp.moveaxis(q.reshape(B, rows, GRID_W, H, dh), 1, 0)
    y = _from_blocks(lax.map(row_block, (jnp.arange(rows, dtype=jnp.int32), q_rows))) @ wo
    yc = (_mha(qc, kc, vc).reshape(B, n_ctx, H * dh) @ wo) if need_ctx else None
    return y, yc


def _gqa_sink(q, k, v, sink, mask=None):
    s = jnp.einsum('bqkgd,bmkd->bkgqm', q, k).astype(jnp.float32) * (HEAD_DIM ** -0.5)
    if mask is not None:
        s = jnp.where(mask, s, NEG_INF)
    b, kv, g, n, _ = s.shape
    s = jnp.concatenate([s, jnp.broadcast_to(sink[None, :, :, None, None], (b, kv, g, n, 1))], axis=-1)
    p = jax.nn.softmax(s, axis=-1)[..., :-1].astype(v.dtype)
    o = jnp.einsum('bkgqm,bmkd->bqkgd', p, v)
    return o.reshape(b, n, -1)


def window_gqa_sink(h, hc, wqkv, wo, sinks, tables, need_ctx):
    B, S, _ = h.shape
    H, KV, dh = SWA_HEADS, SWA_KV_HEADS, HEAD_DIM
    G = H // KV
    n_ctx = hc.shape[1]

    def proj(t):
        n = t.shape[1]
        q, k, v = jnp.split(t @ wqkv, [H * dh, (H + KV) * dh], axis=-1)
        return q.reshape(B, n, H, dh), k.reshape(B, n, KV, dh), v.reshape(B, n, KV, dh)

    q, k, v = proj(h)
    q, k = rope_2d(q, tables), rope_2d(k, tables)
    q = q.reshape(B, S, KV, G, dh)
    qc, kc, vc = proj(hc)
    qc = qc.reshape(B, n_ctx, KV, G, dh)
    sink = sinks.astype(jnp.float32).reshape(KV, G)

    pad = SWA_WINDOW
    kp = jnp.pad(k, ((0, 0), (pad, pad), (0, 0), (0, 0)))
    vp = jnp.pad(v, ((0, 0), (pad, pad), (0, 0), (0, 0)))
    span = Q_BLOCK + 2 * pad
    qi = jnp.arange(Q_BLOCK, dtype=jnp.int32)
    kj = jnp.arange(span, dtype=jnp.int32)
    band = (kj[None, :] >= qi[:, None]) & (kj[None, :] <= qi[:, None] + 2 * SWA_WINDOW)
    ctx_ok = jnp.ones((Q_BLOCK, n_ctx), dtype=bool)

    def block(args):
        n, q_blk = args
        start = n * Q_BLOCK
        k_loc = lax.dynamic_slice_in_dim(kp, start, span, axis=1)
        v_loc = lax.dynamic_slice_in_dim(vp, start, span, axis=1)
        kpos = start - pad + kj
        valid = band & ((kpos >= 0) & (kpos < S))[None, :]
        mask = jnp.concatenate([valid, ctx_ok], axis=-1)
        return _gqa_sink(q_blk, jnp.concatenate([k_loc, kc], axis=1),
                         jnp.concatenate([v_loc, vc], axis=1), sink, mask)

    nb = S // Q_BLOCK
    y = _from_blocks(lax.map(block, (jnp.arange(nb, dtype=jnp.int32), _to_blocks(q, Q_BLOCK)))) @ wo
    yc = (_gqa_sink(qc, kc, vc, sink) @ wo) if need_ctx else None
    return y, yc


def conv_ffn(t, w_up, w_conv, b_conv, w_down):
    n = t.shape[1]
    u = t @ w_up
    half = CONV_W // 2
    up = jnp.pad(u, ((0, 0), (half, half), (0, 0)))
    acc = b_conv
    for tap in range(CONV_W):
        acc = acc + up[:, tap:tap + n] * w_conv[tap]
    a, g = jnp.split(acc, 2, axis=-1)
    return (jax.nn.silu(g) * a) @ w_down


def setup_inputs(seed: int = 0) -> dict:
    key = jax.random.key(seed)
    ks = iter(jax.random.split(key, 32))

    def nrm(shape, scale):
        return jax.random.normal(next(ks), shape, jnp.float32) * scale

    D = D_MODEL
    s = D ** -0.5
    return {
        "x": nrm((BATCH, SEQ, D), 1.0),
        "c": nrm((BATCH, D), 1.0),
        "ctx": nrm((BATCH, CTX_LEN, D), 1.0),
        "c_ctx": nrm((D,), 1.0),
        "ada_w": nrm((DEPTH, D, 6 * D), s),
        "ada_b": nrm((DEPTH, 6 * D), 0.02),
        "norm_mix": 1.0 + nrm((DEPTH, D), 0.05),
        "norm_ffn": 1.0 + nrm((DEPTH, D), 0.05),
        "norm_out": 1.0 + nrm((D,), 0.05),
        "ffn_up": nrm((DEPTH, D, 2 * D_FF), s),
        "ffn_conv": nrm((DEPTH, CONV_W, 2 * D_FF), CONV_W ** -0.5),
        "ffn_conv_b": nrm((DEPTH, 2 * D_FF), 0.02),
        "ffn_down": nrm((DEPTH, D_FF, D), D_FF ** -0.5),
        "a_wqkv": nrm((N_LAYERS_A, D, 3 * D), s),
        "a_wo": nrm((N_LAYERS_A, D, D), s),
        "a_lambda": nrm((N_LAYERS_A, 4, HEAD_DIM), 0.1),
        "a_subln": 1.0 + nrm((N_LAYERS_A, 2 * HEAD_DIM), 0.05),
        "b_wqkv": nrm((N_LAYERS_B, D, 3 * D), s),
        "b_wo": nrm((N_LAYERS_B, D, D), s),
        "b_rpb": nrm((N_LAYERS_B, NA_HEADS, 2 * NA_KH_MAX - 1, 2 * NA_KW - 1), 0.5),
        "c_wqkv": nrm((N_LAYERS_C, D, (SWA_HEADS + 2 * SWA_KV_HEADS) * HEAD_DIM), s),
        "c_wo": nrm((N_LAYERS_C, D, D), s),
        "c_sinks": nrm((N_LAYERS_C, SWA_HEADS), 0.5),
    }


def reference(x, c, ctx, c_ctx, ada_w, ada_b, norm_mix, norm_ffn, norm_out,
              ffn_up, ffn_conv, ffn_conv_b, ffn_down,
              a_wqkv, a_wo, a_lambda, a_subln,
              b_wqkv, b_wo, b_rpb,
              c_wqkv, c_wo, c_sinks):
    S = x.shape[1]
    tables = rope_tables(S, x.dtype)
    xc = ctx
    silu_c = jax.nn.silu(c)
    silu_cc = jax.nn.silu(c_ctx)
    for i in range(DEPTH):
        need_ctx = i < DEPTH - 1
        kind, j = i % N_MIXERS, i // N_MIXERS
        sh1, sc1, g1, sh2, sc2, g2 = jnp.split((silu_c @ ada_w[i] + ada_b[i])[:, None, :], 6, axis=-1)
        csh1, csc1, cg1, csh2, csc2, cg2 = jnp.split(silu_cc @ ada_w[i] + ada_b[i], 6, axis=-1)
        h = rmsnorm(x, norm_mix[i]) * (1.0 + sc1) + sh1
        hc = rmsnorm(xc, norm_mix[i]) * (1.0 + csc1) + csh1
        if kind == 0:
            y, yc = diff_attention(h, hc, a_wqkv[j], a_wo[j], a_lambda[j], a_subln[j], i, tables, need_ctx)
        elif kind == 1:
            y, yc = neighbourhood_attention(h, hc, b_wqkv[j], b_wo[j], b_rpb[j], need_ctx)
        else:
            y, yc = window_gqa_sink(h, hc, c_wqkv[j], c_wo[j], c_sinks[j], tables, need_ctx)
        x = x + g1 * y
        h2 = rmsnorm(x, norm_ffn[i]) * (1.0 + sc2) + sh2
        x = x + g2 * conv_ffn(h2, ffn_up[i], ffn_conv[i], ffn_conv_b[i], ffn_down[i])
        if need_ctx:
            xc = xc + cg1 * yc
            hc2 = rmsnorm(xc, norm_ffn[i]) * (1.0 + csc2) + csh2
            xc = xc + cg2 * conv_ffn(hc2, ffn_up[i], ffn_conv[i], ffn_conv_b[i], ffn_down[i])
    return rmsnorm(x, norm_out)
```

```python
from contextlib import ExitStack
import numpy as np
import concourse.bass as bass
import concourse.mybir as mybir
from concourse.bass_utils import run_bass_kernel_spmd

F32 = mybir.dt.float32
BF16 = mybir.dt.bfloat16
AF = mybir.ActivationFunctionType
ALU = mybir.AluOpType

D = 1024
S = 4096
C = 256
NT = S + C
DFF = 2816
KC = 8
DEPTH = 4
EPS = 1e-6
BLOCKS = [(i * 512, 512, 0) for i in range(8)] + [(S, C, 1)]

VEC_FIELDS = [("nm", 32), ("nf", 32), ("no", 8), ("adab", 4 * 48), ("cw", 4 * 3 * 44), ("cb", 4 * 44),
              ("lam", 2 * 4 * 64), ("subln", 2 * 128), ("sink", 16), ("cc", 16)]
VOFF = {}
_o = 0
for _n, _w in VEC_FIELDS:
    VOFF[_n] = (_o, _w)
    _o += _w
NV = _o


def _fm(a, nch):
    lead = a.shape[:-1]
    a = a.reshape(*lead, nch, 128)
    a = np.moveaxis(a, -1, 0)
    return np.ascontiguousarray(a).reshape(128, -1)


def _pack_vec(inp, b):
    parts = {
        "nm": _fm(inp["norm_mix"], 8), "nf": _fm(inp["norm_ffn"], 8), "no": _fm(inp["norm_out"], 8),
        "adab": _fm(inp["ada_b"], 48), "cw": _fm(inp["ffn_conv"], 44), "cb": _fm(inp["ffn_conv_b"], 44),
        "lam": np.broadcast_to(inp["a_lambda"].reshape(1, -1), (128, 512)),
        "subln": np.broadcast_to(inp["a_subln"].reshape(1, -1), (128, 256)),
        "sink": np.broadcast_to(inp["c_sinks"].reshape(1, -1), (128, 16)),
        "cc": _fm(np.stack([inp["c"][b], inp["c_ctx"]], axis=0), 8).reshape(128, 2, 8).transpose(0, 2, 1).reshape(128, 16),
    }
    v = np.zeros((128, NV), np.float32)
    for n, (o, w) in VOFF.items():
        v[:, o:o + w] = parts[n]
    return v


def _const_tables():
    t = np.arange(S)
    row = (t // 64).astype(np.float32)
    col = (t % 64).astype(np.float32)
    inv = (np.float32(10000.0) ** (-np.arange(16, dtype=np.float32) / np.float32(16))).astype(np.float32)
    ar = (row[:, None] * inv[None, :]).astype(np.float32)
    ac = (col[:, None] * inv[None, :]).astype(np.float32)
    cos = np.zeros((128, S), np.float32)
    sin = np.zeros((128, S), np.float32)
    for p in range(128):
        d = p % 64
        a = ar if d < 32 else ac
        f = d % 16
        cos[p] = np.cos(a[:, f])
        s_ = np.sin(a[:, f])
        sin[p] = -s_ if (d % 32) < 16 else s_
    cst = np.zeros((128, 4, 128), np.float32)
    for m in range(128):
        d = m % 64
        base = m - d
        partner = d + 16 if (d % 32) < 16 else d - 16
        cst[base + partner, 0, m] = 1.0
        cst[m, 1, m] = 1.0
    a_ = np.arange(128)[:, None]
    b_ = np.arange(128)[None, :]
    cst[:, 2, :] = (b_ <= a_)
    cst[:, 3, :] = (a_ <= b_)
    return cos, sin, cst.reshape(128, 512)


def _na_table(rpb):
    colv = np.arange(64)
    cs = np.clip(colv - 8, 0, 48)
    ok = (colv[None, :] >= cs[:, None]) & (colv[None, :] < cs[:, None] + 16)
    dx = np.clip(colv[None, :] - colv[:, None], -15, 15) + 15
    g = rpb[:, :, dx]
    g = np.where(ok[None, None], g, np.float32(-30000.0)).astype(np.float32)
    return np.ascontiguousarray(g.transpose(0, 3, 1, 2)).reshape(16, 64, 15 * 64)


class T:
    __slots__ = ("w", "r", "excl")

    def __init__(self, excl=False):
        self.w = None
        self.r = {}
        self.excl = excl


class Slot:
    def __init__(self, sem, idx):
        self.sem = sem
        self.idx = idx
        self.val = 0


class Eng:
    def __init__(self, h, sem, idx):
        self.h = h
        self.sem = sem
        self.idx = idx
        self.n = 0
        self.seen = {}


class Ctx:
    def __init__(self, nc, es, nslots=56):
        self.nc = nc
        self.sems = {}
        hs = {"pe": nc.tensor, "act": nc.scalar, "dve": nc.vector, "pool": nc.gpsimd, "sp": nc.sync}
        self.E = {}
        i = 0
        for n, h in hs.items():
            sem = es.enter_context(nc.semaphore("e_" + n))
            self.sems[i] = sem
            self.E[n] = Eng(h, sem, i)
            i += 1
        self.slots = []
        for j in range(nslots):
            sem = es.enter_context(nc.semaphore("d%d" % j))
            self.sems[i] = sem
            self.slots.append(Slot(sem, i))
            i += 1
        self.si = 0

    def slot(self):
        s = self.slots[self.si]
        self.si += 1
        return s

    def _waits(self, e, reads, writes, extra=()):
        deps = {}
        for t in reads:
            if t.w is not None and deps.get(t.w[0], 0) < t.w[1]:
                deps[t.w[0]] = t.w[1]
        for t in writes:
            if t.w is not None and t.w[0] != e.idx and deps.get(t.w[0], 0) < t.w[1]:
                deps[t.w[0]] = t.w[1]
            for si, v in t.r.items():
                if si != e.idx and deps.get(si, 0) < v:
                    deps[si] = v
        for si, v in extra:
            if deps.get(si, 0) < v:
                deps[si] = v
        for si, v in deps.items():
            if e.seen.get(si, 0) < v:
                e.h.wait_ge(self.sems[si], v)
                e.seen[si] = v

    @staticmethod
    def _mark(ev, reads, writes):
        for t in writes:
            t.w = ev
            t.r = {}
        for t in reads:
            if t.r.get(ev[0], 0) < ev[1]:
                t.r[ev[0]] = ev[1]

    def op(self, eng, fns, reads=(), writes=()):
        e = self.E[eng]
        if any(t.excl for t in reads):
            writes = list(writes) + [t for t in reads if t.excl]
            reads = [t for t in reads if not t.excl]
        self._waits(e, reads, writes)
        if callable(fns):
            fns = [fns]
        ins = None
        for f in fns:
            ins = f(e.h)
        e.n += 1
        ins.then_inc(e.sem, 1)
        self._mark((e.idx, e.n), reads, writes)

    def dma(self, q, out, in_, slot, reads=(), writes=()):
        e = self.E[q]
        self._waits(e, reads, writes, extra=[(slot.idx, slot.val)] if slot.val else ())
        ins = e.h.dma_start(out=out, in_=in_)
        slot.val += 16
        ins.then_inc(slot.sem, 16)
        self._mark((slot.idx, slot.val), reads, writes)

    def barrier(self):
        evs = [(e.idx, e.n) for e in self.E.values() if e.n] + [(s.idx, s.val) for s in self.slots if s.val]
        for e in self.E.values():
            for si, v in evs:
                if e.seen.get(si, 0) < v and not (si == e.idx):
                    e.h.wait_ge(self.sems[si], v)
                    e.seen[si] = v
        self.si = 0


def MM(out, lhsT, rhs, start=True, stop=True):
    return lambda e: e.matmul(out, lhsT, rhs, start=start, stop=stop)


def TR(out, in_, ident):
    return lambda e: e.transpose(out, in_, ident)


def ACT(out, in_, func, **kw):
    return lambda e: e.activation(out=out, in_=in_, func=func, **kw)


def TT(out, in0, in1, op):
    return lambda e: e.tensor_tensor(out=out, in0=in0, in1=in1, op=op)


def TS(out, in0, s1, s2=None, op0=ALU.mult, op1=None, **kw):
    if op1 is None:
        return lambda e: e.tensor_scalar(out=out, in0=in0, scalar1=s1, scalar2=None, op0=op0, **kw)
    return lambda e: e.tensor_scalar(out=out, in0=in0, scalar1=s1, scalar2=s2, op0=op0, op1=op1, **kw)


def STT(out, in0, scalar, in1, op0, op1, **kw):
    return lambda e: e.scalar_tensor_tensor(out=out, in0=in0, scalar=scalar, in1=in1, op0=op0, op1=op1, **kw)


def CP(out, in_):
    return lambda e: e.tensor_copy(out=out, in_=in_)


def RCP(out, in_):
    return lambda e: e.reciprocal(out=out, in_=in_)


def MS(ap, v):
    return lambda e: e.memset(ap, v)


def kp(ap):
    return ap.rearrange("(k p) t -> p k t", p=128)


class Prog:
    def __init__(self, layers, dbg, nphase=999):
        self.nphase = nphase
        self.layers = list(layers)
        self.dbg = dbg
        nc = self.nc = bass.Bass("TRN2", target_bir_lowering=False)
        dt = nc.dram_tensor
        self.xin = dt("xT", [D, NT], F32, kind="ExternalInput").ap()
        self.vec_d = dt("vec", [128, NV], F32, kind="ExternalInput").ap()
        self.cos_d = dt("cos", [128, S], F32, kind="ExternalInput").ap()
        self.sin_d = dt("sin", [128, S], F32, kind="ExternalInput").ap()
        self.cst_d = dt("cst", [128, 512], F32, kind="ExternalInput").ap()
        self.natab_d = dt("natab", [16, 64, 960], F32, kind="ExternalInput").ap()
        L = self.layers
        nL = len(L)
        self.li = {l: i for i, l in enumerate(L)}
        ja = sorted({l // 3 for l in L if l % 3 == 0})
        self.ja = {j: i for i, j in enumerate(ja)}
        self.wshapes = {
            "ada_w": [nL, D, 6 * D], "ffn_up": [nL, D, 2 * DFF], "ffn_down": [nL, DFF, D],
            "a_wqkv": [len(ja), D, 3 * D] if ja else [1, 8, 8], "a_wo": [len(ja), D, D] if ja else [1, 8, 8],
            "b_wqkv": [1, D, 3 * D] if 1 in L else [1, 8, 8], "b_wo": [1, D, D] if 1 in L else [1, 8, 8],
            "c_wqkv": [1, D, 1536] if 2 in L else [1, 8, 8], "c_wo": [1, D, D] if 2 in L else [1, 8, 8],
        }
        for n_, sh_ in self.wshapes.items():
            setattr(self, n_, dt(n_, sh_, F32, kind="ExternalInput").ap())
        self.outT = dt("outT", [D, S], F32, kind="ExternalOutput").ap()
        self.xs = dt("xs", [D, NT], F32, kind="ExternalOutput" if dbg else "Internal").ap()
        sk = "ExternalOutput" if dbg else "Internal"
        self.hT = dt("hT", [D, NT], BF16, kind=sk).ap()
        self.oT = dt("oT", [D, NT], BF16, kind=sk).ap()
        self.mT = dt("mT", [DFF, NT], BF16, kind=sk).ap()

        with ExitStack() as es:
            self.k = Ctx(nc, es)
            sb = lambda name, shape, dty: es.enter_context(nc.sbuf_tensor(self.U() + name, shape, dty))
            self.vec = sb("vec_sb", [128, NV], F32)
            self.vecT = T()
            self.mod = sb("mod_sb", [128, DEPTH * 48 * 2], F32)
            self.modT = T()
            self.cstb = sb("cst_sb", [128, 512], BF16)
            self.cstT = T()
            self.onesm = sb("onesm", [128, 128], BF16)
            self.ones1 = sb("ones1", [128, 128], BF16)
            self.onesT = T()
            self.small = sb("small", [128, 64], F32)
            self.smallT = T()
            self.gsub = sb("gsub", [128, 256], F32)
            self.gsubT = T()
            self.build()

    def bank(self, es, dty=F32):
        shape = [128, 512] if dty == F32 else [128, 1024]
        return es.enter_context(self.nc.psum_tensor(self.U() + "bank", shape, dty)), T(True)

    def banks(self, es, n, dty=F32):
        bs = [self.bank(es, dty) for _ in range(n)]
        return [b for b, _ in bs], [t for _, t in bs]

    def U(self):
        self._uid = getattr(self, "_uid", 0) + 1
        return "t%d_" % self._uid

    def v(self, name, a=0, b=None):
        o, w = VOFF[name]
        if b is None:
            b = w
        return self.vec[:, o + a:o + b]

    def modcol(self, l, ch, s):
        i = (l * 48 + ch) * 2 + s
        return self.mod[:, i:i + 1]

    def __getattribute__(self, name):
        a = object.__getattribute__(self, name)
        if name.startswith("phase_"):
            def wrapped(*args, **kw):
                self._pc = getattr(self, "_pc", 0) + 1
                if self._pc > self.nphase:
                    return None
                return a(*args, **kw)
            return wrapped
        return a

    def build(self):
        k, nc = self.k, self.nc
        k.dma("sp", self.vec[:], self.vec_d[:, :], k.slot(), writes=[self.vecT])
        k.dma("pool", self.cstb[:], self.cst_d[:, :], k.slot(), writes=[self.cstT])
        k.op("pool", MS(self.onesm[:], 1.0 / 1024.0), writes=[self.onesT])
        k.barrier()
        self.phase_mod()
        k.barrier()
        xsrc = self.xin
        for l in self.layers:
            last = (l == DEPTH - 1)
            need_ctx = not last
            kind, j = l % 3, l // 3
            self.phase_norm(xsrc, l, 1, True)
            k.barrier()
            if kind == 0:
                self.phase_attn_diff(l, j, need_ctx)
                wo = self.a_wo[self.ja[j]]
            elif kind == 1:
                self.phase_attn_na(l, need_ctx)
                wo = self.b_wo[0]
            else:
                self.phase_attn_swa(l, need_ctx)
                wo = self.c_wo[0]
            k.barrier()
            self.phase_wo(l, wo, xsrc, need_ctx)
            xsrc = self.xs
            k.barrier()
            self.phase_norm(xsrc, l, 2, need_ctx)
            k.barrier()
            self.phase_ffnup(l, need_ctx)
            k.barrier()
            self.phase_ffndown(l, need_ctx)
            k.barrier()
        self.phase_norm(xsrc, 0, 0, False)
        k.barrier()

    def phase_mod(self):
        k, nc = self.k, self.nc
        with ExitStack() as es:
            wb = [es.enter_context(nc.sbuf_tensor(self.U() + "adw%d" % i, [128, 8, 512], F32)) for i in range(2)]
            wT = [T(), T()]
            wS = [k.slot(), k.slot()]
            scc = es.enter_context(nc.sbuf_tensor(self.U() + "scc", [128, 16], F32))
            sccT = T()
            psb, psT = self.bank(es)
            ps = psb[:, 0:96].rearrange("p (c s) -> p c s", s=2)
            k.op("act", ACT(scc[:], self.v("cc"), AF.Silu), reads=[self.vecT], writes=[sccT])
            modv = self.mod[:].rearrange("p (l c s) -> p l c s", l=DEPTH, c=48, s=2)
            jobs = [(l, nb) for l in self.layers for nb in range(12)]

            def load(i):
                l, nb = jobs[i]
                k.dma("sp", wb[i % 2][:], kp(self.ada_w[self.li[l]])[:, :, nb * 512:(nb + 1) * 512], wS[i % 2], writes=[wT[i % 2]])
            load(0)
            for i, (l, nb) in enumerate(jobs):
                if i + 1 < len(jobs):
                    load(i + 1)
                for fc in range(4):
                    ch = nb * 4 + fc
                    k.op("pe", [MM(ps[:, ch, :], wb[i % 2][:, kc, fc * 128:(fc + 1) * 128], scc[:, kc * 2:kc * 2 + 2],
                                   start=(kc == 0), stop=(kc == 7)) for kc in range(8)],
                         reads=[wT[i % 2], sccT], writes=[psT])
                if nb == 11:
                    o, _ = VOFF["adab"]
                    for s in range(2):
                        k.op("dve", TT(modv[:, l, :, s], ps[:, :, s], self.vec[:, o + l * 48:o + (l + 1) * 48], ALU.add),
                             reads=[psT, self.vecT], writes=[self.modT])
                    for s in range(2):
                        k.op("dve", STT(modv[:, l, 8:16, s], modv[:, l, 8:16, s], 1.0, self.v("nm", l * 8, l * 8 + 8), ALU.add, ALU.mult),
                             reads=[self.modT, self.vecT], writes=[self.modT])
                        k.op("dve", STT(modv[:, l, 32:40, s], modv[:, l, 32:40, s], 1.0, self.v("nf", l * 8, l * 8 + 8), ALU.add, ALU.mult),
                             reads=[self.modT, self.vecT], writes=[self.modT])
            tmp = es.enter_context(nc.sbuf_tensor(self.U() + "lamtmp", [128, 64], F32))
            acc = es.enter_context(nc.sbuf_tensor(self.U() + "lamacc", [128, 8], F32))
            tT = T()
            import math
            for j in range(2):
                l = 3 * j
                lam_init = 0.8 - 0.6 * math.exp(-0.3 * l)
                for h in range(2):
                    a0 = self.v("lam", (j * 4 + 2 * h) * 64, (j * 4 + 2 * h + 1) * 64)
                    a1 = self.v("lam", (j * 4 + 2 * h + 1) * 64, (j * 4 + 2 * h + 2) * 64)
                    k.op("dve", STT(tmp[:], a0, 1.0, a1, ALU.mult, ALU.mult, accum_out=acc[:, h:h + 1]),
                         reads=[self.vecT], writes=[tT])
                k.op("act", ACT(acc[:, 2:4], acc[:, 0:2], AF.Exp), reads=[tT], writes=[tT])
                k.op("dve", STT(self.small[:, j:j + 1], acc[:, 3:4], -lam_init, acc[:, 2:3], ALU.add, ALU.subtract),
                     reads=[tT], writes=[self.smallT])
                k.op("dve", TS(self.gsub[:, j * 128:(j + 1) * 128], self.v("subln", j * 128, (j + 1) * 128),
                               (1.0 - lam_init) * math.sqrt(128.0)), reads=[self.vecT], writes=[self.gsubT])
            k.op("act", ACT(self.small[:, 8:24], self.v("sink"), AF.Exp), reads=[self.vecT, self.smallT], writes=[self.smallT])

    def phase_norm(self, xsrc, l, which, with_ctx):
        k, nc = self.k, self.nc
        blocks = BLOCKS if with_ctx else BLOCKS[:8]
        odt = F32 if which == 0 else BF16
        with ExitStack() as es:
            sb = lambda name, shape, dty: es.enter_context(nc.sbuf_tensor(self.U() + name, shape, dty))
            xb = [sb("nx%d" % i, [128, 8, 512], F32) for i in range(2)]
            xT_ = [T(), T()]
            xS = [k.slot(), k.slot()]
            sq = [sb("nsq%d" % i, [128, 8, 512], BF16) for i in range(2)]
            sqT = [T(), T()]
            rs = [sb("nrs%d" % i, [128, 512], F32) for i in range(2)]
            rsT = [T(), T()]
            tmp = [sb("ntmp%d" % i, [128, 8, 512], F32) for i in range(2)]
            tmpT = [[T() for _ in range(8)] for _ in range(2)]
            hb = [sb("nh%d" % i, [128, 8, 512], odt) for i in range(2)]
            hT_ = [T(), T()]
            hS = [k.slot(), k.slot()]
            ss, ssT = self.banks(es, 2)

            def load(i):
                t0, N, s = blocks[i]
                k.dma("sp", xb[i % 2][:, :, :N], kp(xsrc)[:, :, t0:t0 + N], xS[i % 2], writes=[xT_[i % 2]])
            load(0)
            for i, (t0, N, s) in enumerate(blocks):
                b = i % 2
                if i + 1 < len(blocks):
                    load(i + 1)
                k.op("dve", TT(sq[b][:, :, :N], xb[b][:, :, :N], xb[b][:, :, :N], ALU.mult), reads=[xT_[b]], writes=[sqT[b]])
                k.op("pe", [MM(ss[b][:, :N], self.onesm[:], sq[b][:, kc, :N], start=(kc == 0), stop=(kc == 7)) for kc in range(8)],
                     reads=[sqT[b], self.onesT], writes=[ssT[b]])
                k.op("act", ACT(rs[b][:, :N], ss[b][:, :N], AF.Ln, bias=EPS, scale=1.0), reads=[ssT[b]], writes=[rsT[b]])
                k.op("act", ACT(rs[b][:, :N], rs[b][:, :N], AF.Exp, scale=-0.5), reads=[rsT[b]], writes=[rsT[b]])
                for kc in range(8):
                    k.op("dve" if kc % 2 == 0 else "pool", TT(tmp[b][:, kc, :N], xb[b][:, kc, :N], rs[b][:, :N], ALU.mult),
                         reads=[xT_[b], rsT[b]], writes=[tmpT[b][kc]])
                for kc in range(8):
                    if which == 0:
                        fn = ACT(hb[b][:, kc, :N], tmp[b][:, kc, :N], AF.Identity, scale=self.v("no", kc, kc + 1))
                    else:
                        ach = (8 if which == 1 else 32) + kc
                        sch = (0 if which == 1 else 24) + kc
                        fn = ACT(hb[b][:, kc, :N], tmp[b][:, kc, :N], AF.Identity, scale=self.modcol(l, ach, s), bias=self.modcol(l, sch, s))
                    k.op("act", fn, reads=[tmpT[b][kc], self.modT, self.vecT], writes=[hT_[b]])
                dst = self.outT if which == 0 else self.hT
                k.dma("sp", kp(dst)[:, :, t0:t0 + N], hb[b][:, :, :N], hS[b], reads=[hT_[b]])

    def phase_wo(self, l, wo_d, xsrc, with_ctx):
        k, nc = self.k, self.nc
        blocks = BLOCKS if with_ctx else BLOCKS[:8]
        with ExitStack() as es:
            sb = lambda name, shape, dty: es.enter_context(nc.sbuf_tensor(self.U() + name, shape, dty))
            wo = sb("wo", [128, 8, 1024], BF16)
            woT = [T() for _ in range(8)]
            for kc in range(8):
                k.dma("pool", wo[:, kc, :], wo_d[kc * 128:(kc + 1) * 128, :], k.slot(), writes=[woT[kc]])
            ob = [sb("wob%d" % i, [128, 8, 512], BF16) for i in range(2)]
            obT = [T(), T()]
            obS = [k.slot(), k.slot()]
            xb = [sb("wxb%d" % i, [128, 8, 512], F32) for i in range(2)]
            xbT = [T(), T()]
            xbS = [k.slot(), k.slot()]
            xn = [sb("wxn%d" % i, [128, 8, 512], F32) for i in range(2)]
            xnT = [T(), T()]
            xnS = [k.slot(), k.slot()]
            ps, psT = self.banks(es, 4)

            def load(i):
                t0, N, s = blocks[i]
                k.dma("sp", ob[i % 2][:, :, :N], kp(self.oT)[:, :, t0:t0 + N], obS[i % 2], writes=[obT[i % 2]])
                k.dma("sp", xb[i % 2][:, :, :N], kp(xsrc)[:, :, t0:t0 + N], xbS[i % 2], writes=[xbT[i % 2]])
            load(0)
            cnt = 0
            for i, (t0, N, s) in enumerate(blocks):
                b = i % 2
                if i + 1 < len(blocks):
                    load(i + 1)
                for m in range(8):
                    p = cnt % 4
                    cnt += 1
                    k.op("pe", [MM(ps[p][:, :N], wo[:, kc, m * 128:(m + 1) * 128], ob[b][:, kc, :N], start=(kc == 0), stop=(kc == 7))
                                for kc in range(8)], reads=woT + [obT[b]], writes=[psT[p]])
                    k.op("dve", STT(xn[b][:, m, :N], ps[p][:, :N], self.modcol(l, 16 + m, s), xb[b][:, m, :N], ALU.mult, ALU.add),
                         reads=[psT[p], xbT[b], self.modT], writes=[xnT[b]])
                k.dma("sp", kp(self.xs)[:, :, t0:t0 + N], xn[b][:, :, :N], xnS[b], reads=[xnT[b]])

    def phase_ffnup(self, l, with_ctx):
        k, nc = self.k, self.nc
        blocks = BLOCKS if with_ctx else BLOCKS[:8]
        ntok = NT if with_ctx else S
        UW = NT + 4
        with ExitStack() as es:
            sb = lambda name, shape, dty: es.enter_context(nc.sbuf_tensor(self.U() + name, shape, dty))
            H = sb("fH", [128, 8, NT], BF16)
            HT = [T() for _ in range(8)]
            for kc in range(8):
                k.dma("sp", H[:, kc, :ntok], self.hT[kc * 128:(kc + 1) * 128, :ntok], k.slot(), writes=[HT[kc]])
            wa = [sb("fwa%d" % i, [128, 8, 256], BF16) for i in range(2)]
            waT = [(T(), T()), (T(), T())]
            waS = [(k.slot(), k.slot()) for _ in range(2)]
            uas = [sb("fua%d" % i, [128, UW], F32) for i in range(2)]
            ugs = [sb("fug%d" % i, [128, UW], F32) for i in range(2)]
            uaTs, ugTs = [T(), T()], [T(), T()]
            aa = sb("faa", [128, UW], F32)
            ag = sb("fag", [128, UW], F32)
            aaT, agT = T(), T()
            mb1 = sb("fmb", [128, NT], BF16)
            mb = [mb1, mb1]
            mbT1 = T()
            mbT = [mbT1, mbT1]
            mbS1 = k.slot()
            mbS = [mbS1, mbS1]
            ps, psT = self.banks(es, 4)
            for i_ in range(2):
                k.op("pool", MS(uas[i_][:], 0.0), writes=[uaTs[i_]])
                k.op("pool", MS(ugs[i_][:], 0.0), writes=[ugTs[i_]])
            up = kp(self.ffn_up[self.li[l]])
            cwo, _ = VOFF["cw"]
            cbo, _ = VOFF["cb"]

            def load(i):
                k.dma("pool", wa[i % 2][:, :, 0:128], up[:, :, i * 128:(i + 1) * 128], waS[i % 2][0], writes=[waT[i % 2][0]])
                k.dma("pool", wa[i % 2][:, :, 128:256], up[:, :, DFF + i * 128:DFF + (i + 1) * 128], waS[i % 2][1], writes=[waT[i % 2][1]])

            def ucol(t0):
                return t0 + 1 if t0 < S else t0 + 3
            load(0)
            cnt = 0
            hi = ucol(ntok - 1) + 1
            for i in range(22):
                b = i % 2
                ua, ug, uaT, ugT = uas[b], ugs[b], uaTs[b], ugTs[b]
                if i + 1 < 22:
                    load(i + 1)
                for (t0, N, s) in blocks:
                    for half, (u, uT) in enumerate(((ua, uaT), (ug, ugT))):
                        p = cnt % 4
                        cnt += 1
                        k.op("pe", [MM(ps[p][:, :N], wa[b][:, kc, half * 128:(half + 1) * 128], H[:, kc, t0:t0 + N],
                                       start=(kc == 0), stop=(kc == 7)) for kc in range(8)],
                             reads=[waT[b][half]] + HT, writes=[psT[p]])
                        c0 = ucol(t0)
                        if half == 0:
                            k.op("act", ACT(u[:, c0:c0 + N], ps[p][:, :N], AF.Copy), reads=[psT[p]], writes=[uT])
                        else:
                            k.op("dve", CP(u[:, c0:c0 + N], ps[p][:, :N]), reads=[psT[p]], writes=[uT])
                for half, (u, uT, a, aT) in enumerate(((ua, uaT, aa, aaT), (ug, ugT, ag, agT))):
                    ch = half * 22 + i
                    w = [self.vec[:, cwo + (l * 3 + tap) * 44 + ch: cwo + (l * 3 + tap) * 44 + ch + 1] for tap in range(3)]
                    bia = self.vec[:, cbo + l * 44 + ch: cbo + l * 44 + ch + 1]
                    k.op("act", ACT(a[:, 1:hi], u[:, 1:hi], AF.Identity, scale=w[1], bias=bia), reads=[uT, self.vecT], writes=[aT])
                    k.op("dve", STT(a[:, 1:hi], u[:, 0:hi - 1], w[0], a[:, 1:hi], ALU.mult, ALU.add), reads=[uT, aT, self.vecT], writes=[aT])
                    k.op("dve", STT(a[:, 1:hi], u[:, 2:hi + 1], w[2], a[:, 1:hi], ALU.mult, ALU.add), reads=[uT, aT, self.vecT], writes=[aT])
                k.op("act", ACT(ag[:, 1:hi], ag[:, 1:hi], AF.Silu), reads=[agT], writes=[agT])
                k.op("pool", TT(mb[b][:, 0:S], ag[:, 1:S + 1], aa[:, 1:S + 1], ALU.mult), reads=[agT, aaT], writes=[mbT[b]])
                if with_ctx:
                    k.op("pool", TT(mb[b][:, S:NT], ag[:, S + 3:NT + 3], aa[:, S + 3:NT + 3], ALU.mult), reads=[agT, aaT], writes=[mbT[b]])
                k.dma("sp", self.mT[i * 128:(i + 1) * 128, :ntok], mb[b][:, :ntok], mbS[b], reads=[mbT[b]])

    def phase_ffndown(self, l, with_ctx):
        k, nc = self.k, self.nc
        blocks = BLOCKS if with_ctx else BLOCKS[:8]
        with ExitStack() as es:
            sb = lambda name, shape, dty: es.enter_context(nc.sbuf_tensor(self.U() + name, shape, dty))
            wd = sb("dwd", [128, 22, 1024], BF16)
            wdT = [T() for _ in range(22)]
            for kc in range(22):
                k.dma("pool", wd[:, kc, :], self.ffn_down[self.li[l]][kc * 128:(kc + 1) * 128, :], k.slot(), writes=[wdT[kc]])
            mb = [sb("dmb%d" % i, [128, 22, 512], BF16) for i in range(2)]
            mbT = [T(), T()]
            mbS = [k.slot(), k.slot()]
            xb = [sb("dxb%d" % i, [128, 8, 512], F32) for i in range(2)]
            xbT = [T(), T()]
            xbS = [k.slot(), k.slot()]
            xn = [sb("dxn%d" % i, [128, 8, 512], F32) for i in range(2)]
            xnT = [T(), T()]
            xnS = [k.slot(), k.slot()]
            ps, psT = self.banks(es, 4)

            def load(i):
                t0, N, s = blocks[i]
                k.dma("sp", mb[i % 2][:, :, :N], kp(self.mT)[:, :, t0:t0 + N], mbS[i % 2], writes=[mbT[i % 2]])
                k.dma("sp", xb[i % 2][:, :, :N], kp(self.xs)[:, :, t0:t0 + N], xbS[i % 2], writes=[xbT[i % 2]])
            load(0)
            cnt = 0
            for i, (t0, N, s) in enumerate(blocks):
                b = i % 2
                if i + 1 < len(blocks):
                    load(i + 1)
                for m in range(8):
                    p = cnt % 4
                    cnt += 1
                    k.op("pe", [MM(ps[p][:, :N], wd[:, kc, m * 128:(m + 1) * 128], mb[b][:, kc, :N], start=(kc == 0), stop=(kc == 21))
                                for kc in range(22)], reads=wdT + [mbT[b]], writes=[psT[p]])
                    k.op("dve", STT(xn[b][:, m, :N], ps[p][:, :N], self.modcol(l, 40 + m, s), xb[b][:, m, :N], ALU.mult, ALU.add),
                         reads=[psT[p], xbT[b], self.modT], writes=[xnT[b]])
                k.dma("sp", kp(self.xs)[:, :, t0:t0 + N], xn[b][:, :, :N], xnS[b], reads=[xnT[b]])

    def load_H(self, es):
        k, nc = self.k, self.nc
        H = es.enter_context(nc.sbuf_tensor(self.U() + "aH", [128, 8, NT], BF16))
        HT = [T() for _ in range(8)]
        for kc in range(8):
            k.dma("sp", H[:, kc, :], self.hT[kc * 128:(kc + 1) * 128, :], k.slot(), writes=[HT[kc]])
        return H, HT

    def proj_fm(self, H, HT, w, wT, wcol, dst, dstT, pq, pqT, rope=None):
        k = self.k
        for bi, (t0, N, s) in enumerate(BLOCKS):
            p = self._pq
            self._pq ^= 1
            k.op("pe", [MM(pq[p][:, :N], w[:, kc, wcol:wcol + 128], H[:, kc, t0:t0 + N], start=(kc == 0), stop=(kc == 7))
                        for kc in range(8)], reads=[wT] + HT, writes=[pqT[p]])
            if rope is None or s == 1:
                k.op("act", ACT(dst[:, t0:t0 + N], pq[p][:, :N], AF.Copy), reads=[pqT[p]], writes=[dstT])
            else:
                cos, sin, cT, qraw, qrawT, pr, prT, t1, t1T, t2, t2T = rope
                k.op("act", ACT(qraw[:, :N], pq[p][:, :N], AF.Copy), reads=[pqT[p]], writes=[qrawT])
                k.op("pe", MM(pr[:, :N], self.cstb[:, 0:128], qraw[:, :N]), reads=[qrawT, self.cstT], writes=[prT])
                k.op("dve", TT(t1[:, :N], pq[p][:, :N], cos[:, t0:t0 + N], ALU.mult), reads=[pqT[p], cT], writes=[t1T])
                k.op("dve", TT(t2[:, :N], pr[:, :N], sin[:, t0:t0 + N], ALU.mult), reads=[prT, cT], writes=[t2T])
                k.op("pool", TT(dst[:, t0:t0 + N], t1[:, :N], t2[:, :N], ALU.add), reads=[t1T, t2T], writes=[dstT])

    def phase_attn_diff(self, l, j, need_ctx):
        k, nc = self.k, self.nc
        self._pq = 0
        with ExitStack() as es:
            sb = lambda name, shape, dty: es.enter_context(nc.sbuf_tensor(self.U() + name, shape, dty))
            pst = lambda name, shape, dty: es.enter_context(nc.psum_tensor(self.U() + name, shape, dty))
            H, HT = self.load_H(es)
            cos = sb("acos", [128, S], F32)
            sin = sb("asin", [128, S], F32)
            cT = T()
            k.dma("sp", cos[:], self.cos_d[:, :], k.slot(), writes=[cT])
            k.dma("sp", sin[:], self.sin_d[:, :], k.slot(), writes=[cT])
            w = [sb("aw%d" % i, [128, 8, 384], BF16) for i in range(2)]
            wT = [[T(), T(), T()], [T(), T(), T()]]
            wS = [[k.slot() for _ in range(3)] for _ in range(2)]
            qT = sb("aq", [128, NT], BF16)
            kT = sb("ak", [128, NT], BF16)
            V = sb("av", [128, 34, 132], BF16)
            qT_, kT_, VT = T(), T(), T()
            qraw = sb("aqraw", [128, 512], BF16)
            t1 = sb("at1", [128, 512], F32)
            t2 = sb("at2", [128, 512], F32)
            qrawT, t1T, t2T = T(), T(), T()
            pT = [sb("apT%d" % i, [128, 512], BF16) for i in range(3)]
            pT_ = [T() for _ in range(3)]
            OTc = [sb("aOT%d" % i, [128, NT], BF16) for i in range(2)]
            OTcT = [T(), T()]
            OTcS = [k.slot(), k.slot()]
            o0n = sb("ao0n", [128, 4, 128], F32)
            o0nT = [T() for _ in range(4)]
            o32 = sb("ao32", [128, 4, 128], F32)
            o32T = [T() for _ in range(4)]
            junk = sb("ajunk", [128, 128], F32)
            on = sb("aon", [128, 4, 128], BF16)
            onT = T()
            sc = sb("asc", [128, 16], F32)
            sc2 = sb("asc2", [128, 8], F32)
            sc2T = T()
            finT = [T() for _ in range(4)]
            pq, pqT = self.banks(es, 2)
            po, poT = self.banks(es, 4)
            pr, prT = self.bank(es)
            ptrb, ptrT = self.bank(es, BF16)
            ptr = ptrb[:, 0:512]
            k.op("pool", MS(V[:, :, 128:129], 1.0), writes=[VT])
            wq = kp(self.a_wqkv[self.ja[j]])
            neg_lam = self.small[:, j:j + 1]
            G = self.gsub[:, j * 128:(j + 1) * 128]
            rope = (cos, sin, cT, qraw, qrawT, pr, prT, t1, t1T, t2, t2T)

            def loadw(c):
                for part in range(3):
                    k.dma("pool", w[c % 2][:, :, part * 128:(part + 1) * 128], wq[:, :, part * D + c * 128: part * D + (c + 1) * 128],
                          wS[c % 2][part], writes=[wT[c % 2][part]])
            loadw(0)
            sidx = 0
            pidx = 0
            for c in range(8):
                if c + 1 < 8:
                    loadw(c + 1)
                wc, wcT = w[c % 2], wT[c % 2]
                self.proj_fm(H, HT, wc, wcT[0], 0, qT, qT_, pq, pqT, rope)
                self.proj_fm(H, HT, wc, wcT[1], 128, kT, kT_, pq, pqT, rope)
                for g0 in range(0, 34, 4):
                    g1 = min(34, g0 + 4)
                    p = self._pq
                    self._pq ^= 1
                    fns = []
                    for tt in range(g0, g1):
                        for kc in range(8):
                            fns.append(MM(pq[p][:, (tt - g0) * 128:(tt - g0 + 1) * 128], H[:, kc, tt * 128:(tt + 1) * 128],
                                          wc[:, kc, 256:384], start=(kc == 0), stop=(kc == 7)))
                    k.op("pe", fns, reads=[wcT[2]] + HT, writes=[pqT[p]])
                    k.op("act", ACT(V[:, g0:g1, 0:128], pq[p][:, 0:(g1 - g0) * 128].rearrange("p (a b) -> p a b", b=128), AF.Copy),
                         reads=[pqT[p]], writes=[VT])
                OT, OTT = OTc[c % 2], OTcT[c % 2]
                qblocks = [(qb * 512, 512, list(range(34))) for qb in range(8)]
                if need_ctx:
                    qblocks.append((S, C, [32, 33]))
                tasks = []
                for (q0, QN, ktiles) in qblocks:
                    for jm in range(2):
                        for ki, kt in enumerate(ktiles):
                            tasks.append((q0, QN, QN // 128, jm, ki, kt, len(ktiles)))

                def stageA(i):
                    q0, QN, nqs, jm, ki, kt, nk = tasks[i]
                    pl = slice(jm * 64, (jm + 1) * 64)
                    sp_, pb = i % 3, i % 3
                    sbk, sbkT = (pq[0], pq[1], pr)[sp_], (pqT[0], pqT[1], prT)[sp_]
                    k.op("pe", MM(sbk[:, :QN], kT[pl, kt * 128:(kt + 1) * 128], qT[pl, q0:q0 + QN]),
                         reads=[kT_, qT_], writes=[sbkT])
                    k.op("act", ACT(pT[pb][:, :QN], sbk[:, :QN], AF.Exp, scale=0.125), reads=[sbkT], writes=[pT_[pb]])

                def stageB(i):
                    q0, QN, nqs, jm, ki, kt, nk = tasks[i]
                    pb = i % 3
                    k.op("pe", [MM(po[qs][:, 0:129], pT[pb][:, qs * 128:(qs + 1) * 128], V[:, kt, 0:129],
                                   start=(ki == 0), stop=(ki == nk - 1)) for qs in range(nqs)],
                         reads=[pT_[pb], VT], writes=poT[:nqs])
                    if ki != nk - 1:
                        return
                    if jm == 0:
                        for qs in range(nqs):
                            k.op("dve", RCP(sc[:, qs:qs + 1], po[qs][:, 128:129]), reads=[poT[qs]], writes=[finT[qs]])
                            k.op("dve", TS(o0n[:, qs, :], po[qs][:, 0:128], sc[:, qs:qs + 1]), reads=[poT[qs], finT[qs]], writes=[o0nT[qs]])
                        return
                    for qs in range(nqs):
                        k.op("dve", RCP(sc[:, 4 + qs:5 + qs], po[qs][:, 128:129]), reads=[poT[qs]], writes=[finT[qs]])
                    for qs in range(nqs):
                        k.op("dve", TT(sc[:, 8 + qs:9 + qs], sc[:, 4 + qs:5 + qs], neg_lam, ALU.mult), reads=[finT[qs], self.smallT], writes=[finT[qs]])
                    for qs in range(nqs):
                        k.op("dve", STT(o32[:, qs, :], po[qs][:, 0:128], sc[:, 8 + qs:9 + qs], o0n[:, qs, :], ALU.mult, ALU.add),
                             reads=[poT[qs], finT[qs], o0nT[qs]], writes=[o32T[qs]])
                    for qs in range(nqs):
                        k.op("dve", STT(junk[:], o32[:, qs, :], 1.0, o32[:, qs, :], ALU.mult, ALU.mult, accum_out=sc[:, 12 + qs:13 + qs]),
                             reads=[o32T[qs]], writes=[finT[qs]])
                    k.op("act", ACT(sc2[:, 0:nqs], sc[:, 12:12 + nqs], AF.Ln, bias=128.0 * EPS, scale=1.0), reads=finT[:nqs], writes=[sc2T])
                    k.op("act", ACT(sc2[:, 4:4 + nqs], sc2[:, 0:nqs], AF.Exp, scale=-0.5), reads=[sc2T], writes=[sc2T])
                    for qs in range(nqs):
                        k.op("dve", STT(on[:, qs, :], o32[:, qs, :], sc2[:, 4 + qs:5 + qs], G, ALU.mult, ALU.mult),
                             reads=[o32T[qs], sc2T, self.gsubT], writes=[onT])
                    k.op("pe", [TR(ptr[:, qs * 128:(qs + 1) * 128], on[:, qs, :], self.cstb[:, 128:256]) for qs in range(nqs)],
                         reads=[onT, self.cstT], writes=[ptrT])
                    k.op("dve", CP(OT[:, q0:q0 + nqs * 128], ptr[:, 0:nqs * 128]), reads=[ptrT], writes=[OTT])

                stageA(0)
                stageA(1)
                for i in range(len(tasks)):
                    if i + 2 < len(tasks):
                        stageA(i + 2)
                    stageB(i)
                ntok = NT if need_ctx else S
                k.dma("sp", self.oT[c * 128:(c + 1) * 128, :ntok], OT[:, :ntok], OTcS[c % 2], reads=[OTT])

    def ctx_dense(self, es_objs, qT, kT, qT_, kT_, pl, vrhs, VT, dv, sink, dst, dstT):
        k = self.k
        pq, pqT, pc, pcT, po2, po2T, sc, scT = es_objs
        for tt in range(2):
            k.op("pe", MM(pq[tt][:, :C], kT[pl, S + tt * 128:S + (tt + 1) * 128], qT[pl, S:NT]), reads=[kT_, qT_], writes=[pqT[tt]])
            k.op("act", ACT(pc[tt][:, :C], pq[tt][:, :C], AF.Exp, scale=0.125), reads=[pqT[tt]], writes=[pcT[tt]])
        for qs in range(2):
            k.op("pe", [MM(po2[:, qs * 128:qs * 128 + dv + 1], pc[tt][:, qs * 128:(qs + 1) * 128], vrhs(tt), start=(tt == 0), stop=(tt == 1))
                        for tt in range(2)], reads=[pcT[0], pcT[1], VT], writes=[po2T])
        for qs in range(2):
            den = po2[:, qs * 128 + dv:qs * 128 + dv + 1]
            if sink is not None:
                k.op("dve", TS(sc[:, 0:1], den, sink, op0=ALU.add), reads=[po2T, self.smallT], writes=[scT])
                k.op("dve", RCP(sc[:, 1:2], sc[:, 0:1]), reads=[scT], writes=[scT])
            else:
                k.op("dve", RCP(sc[:, 1:2], den), reads=[po2T], writes=[scT])
            k.op("dve", TS(dst(qs), po2[:, qs * 128:qs * 128 + dv], sc[:, 1:2]), reads=[po2T, scT], writes=[dstT])

    def phase_attn_na(self, l, need_ctx):
        k, nc = self.k, self.nc
        self._pq = 0
        with ExitStack() as es:
            sb = lambda name, shape, dty: es.enter_context(nc.sbuf_tensor(self.U() + name, shape, dty))
            pst = lambda name, shape, dty: es.enter_context(nc.psum_tensor(self.U() + name, shape, dty))
            H, HT = self.load_H(es)
            w = [sb("bw%d" % i, [128, 8, 384], BF16) for i in range(2)]
            wT = [[T(), T(), T()], [T(), T(), T()]]
            wS = [[k.slot() for _ in range(3)] for _ in range(2)]
            NB = [sb("bnb%d" % i, [64, 2, 960], F32) for i in range(2)]
            NBT = [[T(), T()], [T(), T()]]
            NBS = [[k.slot(), k.slot()], [k.slot(), k.slot()]]
            qT = sb("bq", [128, NT], BF16)
            kT = sb("bk", [128, NT], BF16)
            Vn = sb("bvn", [64, 64, 2, 66], BF16)
            Vc = sb("bvc", [128, 2, 2, 66], BF16)
            qT_, kT_, VT = T(), T(), T()
            ssb = [sb("bss%d" % i, [64, 512], F32) for i in range(2)]
            ssbT = [T(), T()]
            P = [sb("bP%d" % i, [64, 512], BF16) for i in range(2)]
            PT = [T(), T()]
            Pc = [sb("bPc%d" % i, [128, 128], BF16) for i in range(2)]
            PcT = [T(), T()]
            otok = [sb("bot%d" % i, [64, 4, 128], BF16) for i in range(2)]
            otokT = [T(), T()]
            oc = sb("boc", [128, 2, 128], BF16)
            ocT = T()
            pc = [sb("bpc%d" % i, [128, C], BF16) for i in range(2)]
            pcT = [T(), T()]
            sc = sb("bsc", [128, 8], F32)
            scT = T()
            rc = sb("brc", [64, 8], F32)
            rcT = T()
            OTc = [sb("bOT%d" % i, [128, NT], BF16) for i in range(2)]
            OTcT = [T(), T()]
            OTcS = [k.slot(), k.slot()]
            pq, pqT = self.banks(es, 2)
            pob_, poT = self.banks(es, 2)
            po = [b[0:64, :].rearrange("p (a b) -> p a b", b=128) for b in pob_]
            po2b, po2T = self.bank(es)
            po2 = po2b[:, 0:256]
            pscb, pscT = self.banks(es, 2)
            psc = [b[:, 0:128] for b in pscb]
            ptrb, ptrT = self.bank(es, BF16)
            ptr = ptrb[:, 0:256]
            k.op("pool", MS(Vn[:, :, :, 64:65], 1.0), writes=[VT])
            k.op("pool", MS(Vc[:, :, :, 64:65], 1.0), writes=[VT])
            wq = kp(self.b_wqkv[0])
            ident = self.cstb[:, 128:256]

            def loadw(c):
                for part in range(3):
                    k.dma("pool", w[c % 2][:, :, part * 128:(part + 1) * 128], wq[:, :, part * D + c * 128: part * D + (c + 1) * 128],
                          wS[c % 2][part], writes=[wT[c % 2][part]])
                for jh in range(2):
                    k.dma("sp", NB[c % 2][:, jh, :], self.natab_d[2 * c + jh], NBS[c % 2][jh], writes=[NBT[c % 2][jh]])
            loadw(0)
            sidx = 0
            for c in range(8):
                if c + 1 < 8:
                    loadw(c + 1)
                wc, wcT = w[c % 2], wT[c % 2]
                self.proj_fm(H, HT, wc, wcT[0], 0, qT, qT_, pq, pqT, None)
                self.proj_fm(H, HT, wc, wcT[1], 128, kT, kT_, pq, pqT, None)
                for r0 in range(0, 64, 4):
                    p = self._pq
                    self._pq ^= 1
                    fns = []
                    for rr in range(4):
                        for kc in range(8):
                            fns.append(MM(pq[p][0:64, rr * 128:(rr + 1) * 128], H[:, kc, (r0 + rr) * 64:(r0 + rr + 1) * 64],
                                          wc[:, kc, 256:384], start=(kc == 0), stop=(kc == 7)))
                    k.op("pe", fns, reads=[wcT[2]] + HT, writes=[pqT[p]])
                    k.op("act", ACT(Vn[:, r0:r0 + 4, :, 0:64], pq[p][0:64, :].rearrange("p (a b c) -> p a b c", a=4, b=2), AF.Copy),
                         reads=[pqT[p]], writes=[VT])
                p = self._pq
                self._pq ^= 1
                fns = []
                for tt in range(2):
                    for kc in range(8):
                        fns.append(MM(pq[p][:, tt * 128:(tt + 1) * 128], H[:, kc, S + tt * 128:S + (tt + 1) * 128],
                                      wc[:, kc, 256:384], start=(kc == 0), stop=(kc == 7)))
                k.op("pe", fns, reads=[wcT[2]] + HT, writes=[pqT[p]])
                k.op("act", ACT(Vc[:, :, :, 0:64], pq[p][:, 0:256].rearrange("p (a b c) -> p a b c", a=2, b=2), AF.Copy),
                     reads=[pqT[p]], writes=[VT])
                OT, OTT = OTc[c % 2], OTcT[c % 2]
                nbt = NB[c % 2]
                tasks = [(rg, jh, rr) for rg in range(16) for jh in range(2) for rr in range(4)]

                def geo(i):
                    rg, jh, rr = tasks[i]
                    r = rg * 4 + rr
                    rs_ = min(max(r - 4, 0), 56)
                    return rg, jh, rr, r, rs_, rs_ - r + 7, slice(jh * 64, (jh + 1) * 64), i % 2

                def stageA(i):
                    rg, jh, rr, r, rs_, dy0, pl, sp_ = geo(i)
                    k.op("pe", [MM(pq[sp_][0:64, a * 64:(a + 1) * 64], kT[pl, (rs_ + a) * 64:(rs_ + a + 1) * 64], qT[pl, r * 64:(r + 1) * 64])
                                for a in range(8)], reads=[kT_, qT_], writes=[pqT[sp_]])
                    k.op("pe", [MM(psc[sp_][:, tt * 64:(tt + 1) * 64], kT[pl, S + tt * 128:S + (tt + 1) * 128], qT[pl, r * 64:(r + 1) * 64])
                                for tt in range(2)], reads=[kT_, qT_], writes=[pscT[sp_]])
                    k.op("dve", STT(ssb[sp_][:], pq[sp_][0:64, :], 0.125, nbt[:, jh, dy0 * 64:(dy0 + 8) * 64], ALU.mult, ALU.add),
                         reads=[pqT[sp_], NBT[c % 2][jh]], writes=[ssbT[sp_]])
                    k.op("act", ACT(P[sp_][:], ssb[sp_][:], AF.Exp), reads=[ssbT[sp_]], writes=[PT[sp_]])
                    k.op("act", ACT(Pc[sp_][:], psc[sp_][:], AF.Exp, scale=0.125), reads=[pscT[sp_]], writes=[PcT[sp_]])

                def stageB(i):
                    rg, jh, rr, r, rs_, dy0, pl, sp_ = geo(i)
                    ob, pob = rg % 2, jh
                    fns = [MM(po[pob][:, rr, 0:65], P[sp_][:, a * 64:(a + 1) * 64], Vn[:, rs_ + a, jh, 0:65], start=(a == 0), stop=False)
                           for a in range(8)]
                    fns += [MM(po[pob][:, rr, 0:65], Pc[sp_][:, tt * 64:(tt + 1) * 64], Vc[:, tt, jh, 0:65], start=False, stop=(tt == 1))
                            for tt in range(2)]
                    k.op("pe", fns, reads=[PT[sp_], PcT[sp_], VT], writes=[poT[pob]])
                    if rr != 3:
                        return
                    k.op("dve", RCP(rc[:, jh * 4:jh * 4 + 4], po[pob][:, :, 64]), reads=[poT[pob]], writes=[rcT])
                    for r4 in range(4):
                        k.op("dve", TS(otok[ob][:, r4, jh * 64:(jh + 1) * 64], po[pob][:, r4, 0:64], rc[:, jh * 4 + r4:jh * 4 + r4 + 1]),
                             reads=[poT[pob], rcT], writes=[otokT[ob]])
                    if jh != 1:
                        return
                    k.op("pe", [TR(ptr[:, r4 * 64:(r4 + 1) * 64], otok[ob][:, r4, :], ident[0:64, 0:64]) for r4 in range(4)],
                         reads=[otokT[ob], self.cstT], writes=[ptrT])
                    k.op("dve", CP(OT[:, rg * 256:(rg + 1) * 256], ptr[:, 0:256]), reads=[ptrT], writes=[OTT])

                stageA(0)
                for i in range(len(tasks)):
                    if i + 1 < len(tasks):
                        stageA(i + 1)
                    stageB(i)
                if need_ctx:
                    for jh in range(2):
                        pl = slice(jh * 64, (jh + 1) * 64)
                        self.ctx_dense((pq, pqT, pc, pcT, po2, po2T, sc, scT), qT, kT, qT_, kT_, pl,
                                       (lambda tt, jh=jh: Vc[:, tt, jh, 0:65]), VT, 64, None,
                                       (lambda qs, jh=jh: oc[:, qs, jh * 64:(jh + 1) * 64]), ocT)
                    for qs in range(2):
                        k.op("pe", TR(ptr[:, 0:128], oc[:, qs, :], ident), reads=[ocT, self.cstT], writes=[ptrT])
                        k.op("dve", CP(OT[:, S + qs * 128:S + (qs + 1) * 128], ptr[:, 0:128]), reads=[ptrT], writes=[OTT])
                ntok = NT if need_ctx else S
                k.dma("sp", self.oT[c * 128:(c + 1) * 128, :ntok], OT[:, :ntok], OTcS[c % 2], reads=[OTT])

    def phase_attn_swa(self, l, need_ctx):
        k, nc = self.k, self.nc
        self._pq = 0
        with ExitStack() as es:
            sb = lambda name, shape, dty: es.enter_context(nc.sbuf_tensor(self.U() + name, shape, dty))
            pst = lambda name, shape, dty: es.enter_context(nc.psum_tensor(self.U() + name, shape, dty))
            H, HT = self.load_H(es)
            cos = sb("ccos", [128, S], F32)
            sin = sb("csin", [128, S], F32)
            cT = T()
            k.dma("sp", cos[:], self.cos_d[:, :], k.slot(), writes=[cT])
            k.dma("sp", sin[:], self.sin_d[:, :], k.slot(), writes=[cT])
            w = [sb("cw%d" % i, [128, 8, 384], BF16) for i in range(2)]
            wT = [[T(), T(), T(), T()], [T(), T(), T(), T()]]
            wS = [[k.slot() for _ in range(4)] for _ in range(2)]
            qT = sb("cq", [128, NT], BF16)
            kT = sb("ck", [128, NT], BF16)
            V = sb("cv", [128, 34, 66], BF16)
            qT_, kT_, VT = T(), T(), T()
            qraw = sb("cqraw", [128, 512], BF16)
            t1 = sb("ct1", [128, 512], F32)
            t2 = sb("ct2", [128, 512], F32)
            qrawT, t1T, t2T = T(), T(), T()
            Pl = [sb("cPl%d" % i, [128, 3, 128], BF16) for i in range(2)]
            PlT = [T(), T()]
            Pcx = [sb("cPc%d" % i, [128, 2, 128], BF16) for i in range(2)]
            PcxT = [T(), T()]
            on = [sb("con%d" % i, [128, 128], BF16) for i in range(2)]
            onT = [T(), T()]
            oc = sb("coc", [128, 2, 128], BF16)
            ocT = T()
            pc = [sb("cpc%d" % i, [128, C], BF16) for i in range(2)]
            pcT = [T(), T()]
            sc = sb("csc", [128, 8], F32)
            scT = T()
            OTc = [sb("cOT%d" % i, [128, NT], BF16) for i in range(2)]
            OTcT = [T(), T()]
            OTcS = [k.slot(), k.slot()]
            pq, pqT = self.banks(es, 2)
            pr, prT = self.bank(es)
            pcxb, pcxT = self.banks(es, 2)
            pcx = [b[:, 0:256] for b in pcxb]
            pob_, poT = self.banks(es, 2)
            po = [b[:, 0:256].rearrange("p (a b) -> p a b", b=128) for b in pob_]
            ptrb, ptrT = self.bank(es, BF16)
            ptr = ptrb[:, 0:128]
            k.op("pool", MS(V[:, :, 64:65], 1.0), writes=[VT])
            wq = kp(self.c_wqkv[0])
            ident = self.cstb[:, 128:256]
            mprev = self.cstb[:, 256:384]
            mnext = self.cstb[:, 384:512]
            rope = (cos, sin, cT, qraw, qrawT, pr, prT, t1, t1T, t2, t2T)

            def loadw(c):
                kv = c // 2
                b = c % 2
                k.dma("pool", w[b][:, :, 0:128], wq[:, :, c * 128:(c + 1) * 128], wS[b][0], writes=[wT[b][0]])
                k.dma("pool", w[b][:, :, 128:192], wq[:, :, D + kv * 64:D + (kv + 1) * 64], wS[b][1], writes=[wT[b][1]])
                k.dma("pool", w[b][:, :, 192:256], wq[:, :, D + kv * 64:D + (kv + 1) * 64], wS[b][2], writes=[wT[b][2]])
                k.dma("pool", w[b][:, :, 256:320], wq[:, :, D + 256 + kv * 64:D + 256 + (kv + 1) * 64], wS[b][3], writes=[wT[b][3]])
            loadw(0)
            sidx = 0
            for c in range(8):
                if c + 1 < 8:
                    loadw(c + 1)
                wc, wcT = w[c % 2], wT[c % 2]
                self.proj_fm(H, HT, wc, wcT[0], 0, qT, qT_, pq, pqT, rope)
                HT2 = HT + [wcT[2]]
                self.proj_fm(H, HT2, wc, wcT[1], 128, kT, kT_, pq, pqT, rope)
                for g0 in range(0, 34, 8):
                    g1 = min(34, g0 + 8)
                    p = self._pq
                    self._pq ^= 1
                    fns = []
                    for tt in range(g0, g1):
                        for kc in range(8):
                            fns.append(MM(pq[p][:, (tt - g0) * 64:(tt - g0 + 1) * 64], H[:, kc, tt * 128:(tt + 1) * 128],
                                          wc[:, kc, 256:320], start=(kc == 0), stop=(kc == 7)))
                    k.op("pe", fns, reads=[wcT[3]] + HT, writes=[pqT[p]])
                    k.op("act", ACT(V[:, g0:g1, 0:64], pq[p][:, 0:(g1 - g0) * 64].rearrange("p (a b) -> p a b", b=64), AF.Copy),
                         reads=[pqT[p]], writes=[VT])
                OT, OTT = OTc[c % 2], OTcT[c % 2]
                tasks = [(n, jh) for n in range(32) for jh in range(2)]

                def stageA(i):
                    n, jh = tasks[i]
                    pl = slice(jh * 64, (jh + 1) * 64)
                    sp_ = i % 2
                    tiles = [t for t in (n - 1, n, n + 1) if 0 <= t < 32]
                    nl = len(tiles)
                    k.op("pe", [MM(pq[sp_][:, a * 128:(a + 1) * 128], kT[pl, kt * 128:(kt + 1) * 128], qT[pl, n * 128:(n + 1) * 128])
                                for a, kt in enumerate(tiles)], reads=[kT_, qT_], writes=[pqT[sp_]])
                    k.op("pe", [MM(pcx[sp_][:, tt * 128:(tt + 1) * 128], kT[pl, S + tt * 128:S + (tt + 1) * 128], qT[pl, n * 128:(n + 1) * 128])
                                for tt in range(2)], reads=[kT_, qT_], writes=[pcxT[sp_]])
                    k.op("act", ACT(Pl[sp_][:, 0:nl, :], pq[sp_][:, 0:nl * 128].rearrange("p (a b) -> p a b", b=128), AF.Exp, scale=0.125),
                         reads=[pqT[sp_]], writes=[PlT[sp_]])
                    k.op("act", ACT(Pcx[sp_][:, :, :], pcx[sp_][:, :].rearrange("p (a b) -> p a b", b=128), AF.Exp, scale=0.125),
                         reads=[pcxT[sp_]], writes=[PcxT[sp_]])
                    for a, kt in enumerate(tiles):
                        if kt == n - 1:
                            k.op("pool", TT(Pl[sp_][:, a, :], Pl[sp_][:, a, :], mprev, ALU.mult), reads=[PlT[sp_], self.cstT], writes=[PlT[sp_]])
                        elif kt == n + 1:
                            k.op("pool", TT(Pl[sp_][:, a, :], Pl[sp_][:, a, :], mnext, ALU.mult), reads=[PlT[sp_], self.cstT], writes=[PlT[sp_]])

                def stageB(i):
                    n, jh = tasks[i]
                    sp_, ob = i % 2, n % 2
                    hq = 2 * c + jh
                    tiles = [t for t in (n - 1, n, n + 1) if 0 <= t < 32]
                    fns = [MM(po[ob][:, jh, 0:65], Pl[sp_][:, a, :], V[:, kt, 0:65], start=(a == 0), stop=False) for a, kt in enumerate(tiles)]
                    fns += [MM(po[ob][:, jh, 0:65], Pcx[sp_][:, tt, :], V[:, 32 + tt, 0:65], start=False, stop=(tt == 1)) for tt in range(2)]
                    k.op("pe", fns, reads=[PlT[sp_], PcxT[sp_], VT], writes=[poT[ob]])
                    k.op("dve", TS(sc[:, 2 * jh:2 * jh + 1], po[ob][:, jh, 64:65], self.small[:, 8 + hq:9 + hq], op0=ALU.add),
                         reads=[poT[ob], self.smallT], writes=[scT])
                    k.op("dve", RCP(sc[:, 2 * jh + 1:2 * jh + 2], sc[:, 2 * jh:2 * jh + 1]), reads=[scT], writes=[scT])
                    k.op("dve", TS(on[ob][:, jh * 64:(jh + 1) * 64], po[ob][:, jh, 0:64], sc[:, 2 * jh + 1:2 * jh + 2]), reads=[poT[ob], scT], writes=[onT[ob]])
                    if jh != 1:
                        return
                    k.op("pe", TR(ptr[:], on[ob][:], ident), reads=[onT[ob], self.cstT], writes=[ptrT])
                    k.op("dve", CP(OT[:, n * 128:(n + 1) * 128], ptr[:]), reads=[ptrT], writes=[OTT])

                stageA(0)
                for i in range(len(tasks)):
                    if i + 1 < len(tasks):
                        stageA(i + 1)
                    stageB(i)
                if need_ctx:
                    for jh in range(2):
                        pl = slice(jh * 64, (jh + 1) * 64)
                        hq = 2 * c + jh
                        self.ctx_dense((pq, pqT, pc, pcT, pr, prT, sc, scT), qT, kT, qT_, kT_, pl,
                                       (lambda tt: V[:, 32 + tt, 0:65]), VT, 64, self.small[:, 8 + hq:9 + hq],
                                       (lambda qs, jh=jh: oc[:, qs, jh * 64:(jh + 1) * 64]), ocT)
                    for qs in range(2):
                        k.op("pe", TR(ptr[:], oc[:, qs, :], ident), reads=[ocT, self.cstT], writes=[ptrT])
                        k.op("dve", CP(OT[:, S + qs * 128:S + (qs + 1) * 128], ptr[:]), reads=[ptrT], writes=[OTT])
                ntok = NT if need_ctx else S
                k.dma("sp", self.oT[c * 128:(c + 1) * 128, :ntok], OT[:, :ntok], OTcS[c % 2], reads=[OTT])


_CACHE = {}


def _get_prog(layers, dbg):
    key = (tuple(layers), dbg)
    if key not in _CACHE:
        _CACHE[key] = Prog(layers, dbg)
    return _CACHE[key]


def _in_maps(inp, xT_list, prog):
    cos, sin, cst = _const_tables()
    natab = _na_table(np.asarray(inp["b_rpb"][0], np.float32))
    L = prog.layers
    ja = sorted(prog.ja)
    shared = {}
    for n, sh in prog.wshapes.items():
        if sh == [1, 8, 8]:
            shared[n] = np.zeros(sh, np.float32)
        elif n in ("ada_w", "ffn_up", "ffn_down"):
            shared[n] = np.ascontiguousarray(np.asarray(inp[n], np.float32)[L])
        elif n in ("a_wqkv", "a_wo"):
            shared[n] = np.ascontiguousarray(np.asarray(inp[n], np.float32)[ja])
        else:
            shared[n] = np.ascontiguousarray(np.asarray(inp[n], np.float32))
    maps = []
    for b, xT in enumerate(xT_list):
        m = {"xT": xT, "vec": _pack_vec(inp, b), "cos": cos, "sin": sin, "cst": cst, "natab": natab}
        m.update(shared)
        maps.append(m)
    return maps


def kernel(**inp):
    inp = {n: np.asarray(v) for n, v in inp.items()}
    B = inp["x"].shape[0]
    xT_list = [np.ascontiguousarray(np.concatenate([inp["x"][b], inp["ctx"][b]], axis=0).T.astype(np.float32)) for b in range(B)]
    prog = _get_prog(range(DEPTH), False)
    res = run_bass_kernel_spmd(prog.nc, _in_maps(inp, xT_list, prog), core_ids=list(range(B)))
    out = np.stack([np.ascontiguousarray(res.results[b]["outT"].T) for b in range(B)], axis=0)
    return out.astype(np.float32)
```

```python
from contextlib import ExitStack
import numpy as np
import concourse.bass as bass
import concourse.mybir as mybir
from concourse.bass_utils import run_bass_kernel_spmd

F32 = mybir.dt.float32
BF16 = mybir.dt.bfloat16
AF = mybir.ActivationFunctionType
ALU = mybir.AluOpType

D = 1024
S = 4096
C = 256
NT = S + C
DFF = 2816
KC = 8
DEPTH = 4
EPS = 1e-6
BLOCKS = [(i * 512, 512, 0) for i in range(8)] + [(S, C, 1)]

VEC_FIELDS = [("nm", 32), ("nf", 32), ("no", 8), ("adab", 4 * 48), ("cw", 4 * 3 * 44), ("cb", 4 * 44),
              ("lam", 2 * 4 * 64), ("subln", 2 * 128), ("sink", 16), ("cc", 16)]
VOFF = {}
_o = 0
for _n, _w in VEC_FIELDS:
    VOFF[_n] = (_o, _w)
    _o += _w
NV = _o


def _fm(a, nch):
    lead = a.shape[:-1]
    a = a.reshape(*lead, nch, 128)
    a = np.moveaxis(a, -1, 0)
    return np.ascontiguousarray(a).reshape(128, -1)


def _pack_vec(inp, b):
    parts = {
        "nm": _fm(inp["norm_mix"], 8), "nf": _fm(inp["norm_ffn"], 8), "no": _fm(inp["norm_out"], 8),
        "adab": _fm(inp["ada_b"], 48), "cw": _fm(inp["ffn_conv"], 44), "cb": _fm(inp["ffn_conv_b"], 44),
        "lam": np.broadcast_to(inp["a_lambda"].reshape(1, -1), (128, 512)),
        "subln": np.broadcast_to(inp["a_subln"].reshape(1, -1), (128, 256)),
        "sink": np.broadcast_to(inp["c_sinks"].reshape(1, -1), (128, 16)),
        "cc": _fm(np.stack([inp["c"][b], inp["c_ctx"]], axis=0), 8).reshape(128, 2, 8).transpose(0, 2, 1).reshape(128, 16),
    }
    v = np.zeros((128, NV), np.float32)
    for n, (o, w) in VOFF.items():
        v[:, o:o + w] = parts[n]
    return v


def _const_tables():
    t = np.arange(S)
    row = (t // 64).astype(np.float32)
    col = (t % 64).astype(np.float32)
    inv = (np.float32(10000.0) ** (-np.arange(16, dtype=np.float32) / np.float32(16))).astype(np.float32)
    ar = (row[:, None] * inv[None, :]).astype(np.float32)
    ac = (col[:, None] * inv[None, :]).astype(np.float32)
    cos = np.zeros((128, S), np.float32)
    sin = np.zeros((128, S), np.float32)
    for p in range(128):
        d = p % 64
        a = ar if d < 32 else ac
        f = d % 16
        cos[p] = np.cos(a[:, f])
        s_ = np.sin(a[:, f])
        sin[p] = -s_ if (d % 32) < 16 else s_
    cst = np.zeros((128, 4, 128), np.float32)
    for m in range(128):
        d = m % 64
        base = m - d
        partner = d + 16 if (d % 32) < 16 else d - 16
        cst[base + partner, 0, m] = 1.0
        cst[m, 1, m] = 1.0
    a_ = np.arange(128)[:, None]
    b_ = np.arange(128)[None, :]
    cst[:, 2, :] = (b_ <= a_)
    cst[:, 3, :] = (a_ <= b_)
    return cos, sin, cst.reshape(128, 512)


def _na_table(rpb):
    colv = np.arange(64)
    cs = np.clip(colv - 8, 0, 48)
    ok = (colv[None, :] >= cs[:, None]) & (colv[None, :] < cs[:, None] + 16)
    dx = np.clip(colv[None, :] - colv[:, None], -15, 15) + 15
    g = rpb[:, :, dx]
    g = np.where(ok[None, None], g, np.float32(-30000.0)).astype(np.float32)
    return np.ascontiguousarray(g.transpose(0, 3, 1, 2)).reshape(16, 64, 15 * 64)


class T:
    __slots__ = ("w", "r", "excl")

    def __init__(self, excl=False):
        self.w = None
        self.r = {}
        self.excl = excl


class Slot:
    def __init__(self, sem, idx):
        self.sem = sem
        self.idx = idx
        self.val = 0


class Eng:
    def __init__(self, h, sem, idx):
        self.h = h
        self.sem = sem
        self.idx = idx
        self.n = 0
        self.seen = {}


class Ctx:
    def __init__(self, nc, es, nslots=56):
        self.nc = nc
        self.sems = {}
        hs = {"pe": nc.tensor, "act": nc.scalar, "dve": nc.vector, "pool": nc.gpsimd, "sp": nc.sync}
        self.E = {}
        i = 0
        for n, h in hs.items():
            sem = es.enter_context(nc.semaphore("e_" + n))
            self.sems[i] = sem
            self.E[n] = Eng(h, sem, i)
            i += 1
        self.slots = []
        for j in range(nslots):
            sem = es.enter_context(nc.semaphore("d%d" % j))
            self.sems[i] = sem
            self.slots.append(Slot(sem, i))
            i += 1
        self.si = 0

    def slot(self):
        s = self.slots[self.si]
        self.si += 1
        return s

    def _waits(self, e, reads, writes, extra=()):
        deps = {}
        for t in reads:
            if t.w is not None and deps.get(t.w[0], 0) < t.w[1]:
                deps[t.w[0]] = t.w[1]
        for t in writes:
            if t.w is not None and t.w[0] != e.idx and deps.get(t.w[0], 0) < t.w[1]:
                deps[t.w[0]] = t.w[1]
            for si, v in t.r.items():
                if si != e.idx and deps.get(si, 0) < v:
                    deps[si] = v
        for si, v in extra:
            if deps.get(si, 0) < v:
                deps[si] = v
        for si, v in deps.items():
            if e.seen.get(si, 0) < v:
                e.h.wait_ge(self.sems[si], v)
                e.seen[si] = v

    @staticmethod
    def _mark(ev, reads, writes):
        for t in writes:
            t.w = ev
            t.r = {}
        for t in reads:
            if t.r.get(ev[0], 0) < ev[1]:
                t.r[ev[0]] = ev[1]

    def op(self, eng, fns, reads=(), writes=()):
        e = self.E[eng]
        if any(t.excl for t in reads):
            writes = list(writes) + [t for t in reads if t.excl]
            reads = [t for t in reads if not t.excl]
        self._waits(e, reads, writes)
        if callable(fns):
            fns = [fns]
        ins = None
        for f in fns:
            ins = f(e.h)
        e.n += 1
        ins.then_inc(e.sem, 1)
        self._mark((e.idx, e.n), reads, writes)

    def dma(self, q, out, in_, slot, reads=(), writes=()):
        e = self.E[q]
        self._waits(e, reads, writes, extra=[(slot.idx, slot.val)] if slot.val else ())
        ins = e.h.dma_start(out=out, in_=in_)
        slot.val += 16
        ins.then_inc(slot.sem, 16)
        self._mark((slot.idx, slot.val), reads, writes)

    def barrier(self):
        evs = [(e.idx, e.n) for e in self.E.values() if e.n] + [(s.idx, s.val) for s in self.slots if s.val]
        for e in self.E.values():
            for si, v in evs:
                if e.seen.get(si, 0) < v and not (si == e.idx):
                    e.h.wait_ge(self.sems[si], v)
                    e.seen[si] = v
        self.si = 0


def MM(out, lhsT, rhs, start=True, stop=True):
    return lambda e: e.matmul(out, lhsT, rhs, start=start, stop=stop)


def TR(out, in_, ident):
    return lambda e: e.transpose(out, in_, ident)


def ACT(out, in_, func, **kw):
    return lambda e: e.activation(out=out, in_=in_, func=func, **kw)


def TT(out, in0, in1, op):
    return lambda e: e.tensor_tensor(out=out, in0=in0, in1=in1, op=op)


def TS(out, in0, s1, s2=None, op0=ALU.mult, op1=None, **kw):
    if op1 is None:
        return lambda e: e.tensor_scalar(out=out, in0=in0, scalar1=s1, scalar2=None, op0=op0, **kw)
    return lambda e: e.tensor_scalar(out=out, in0=in0, scalar1=s1, scalar2=s2, op0=op0, op1=op1, **kw)


def STT(out, in0, scalar, in1, op0, op1, **kw):
    return lambda e: e.scalar_tensor_tensor(out=out, in0=in0, scalar=scalar, in1=in1, op0=op0, op1=op1, **kw)


def CP(out, in_):
    return lambda e: e.tensor_copy(out=out, in_=in_)


def RCP(out, in_):
    return lambda e: e.reciprocal(out=out, in_=in_)


def MS(ap, v):
    return lambda e: e.memset(ap, v)


def kp(ap):
    return ap.rearrange("(k p) t -> p k t", p=128)


class Prog:
    def __init__(self, layers, dbg, nphase=999):
        self.nphase = nphase
        self.layers = list(layers)
        self.dbg = dbg
        nc = self.nc = bass.Bass("TRN2", target_bir_lowering=False)
        dt = nc.dram_tensor
        self.xin = dt("xT", [D, NT], F32, kind="ExternalInput").ap()
        self.vec_d = dt("vec", [128, NV], F32, kind="ExternalInput").ap()
        self.cos_d = dt("cos", [128, S], F32, kind="ExternalInput").ap()
        self.sin_d = dt("sin", [128, S], F32, kind="ExternalInput").ap()
        self.cst_d = dt("cst", [128, 512], F32, kind="ExternalInput").ap()
        self.natab_d = dt("natab", [16, 64, 960], F32, kind="ExternalInput").ap()
        L = self.layers
        nL = len(L)
        self.li = {l: i for i, l in enumerate(L)}
        ja = sorted({l // 3 for l in L if l % 3 == 0})
        self.ja = {j: i for i, j in enumerate(ja)}
        self.wshapes = {
            "ada_w": [nL, D, 6 * D], "ffn_up": [nL, D, 2 * DFF], "ffn_down": [nL, DFF, D],
            "a_wqkv": [len(ja), D, 3 * D] if ja else [1, 8, 8], "a_wo": [len(ja), D, D] if ja else [1, 8, 8],
            "b_wqkv": [1, D, 3 * D] if 1 in L else [1, 8, 8], "b_wo": [1, D, D] if 1 in L else [1, 8, 8],
            "c_wqkv": [1, D, 1536] if 2 in L else [1, 8, 8], "c_wo": [1, D, D] if 2 in L else [1, 8, 8],
        }
        for n_, sh_ in self.wshapes.items():
            setattr(self, n_, dt(n_, sh_, F32, kind="ExternalInput").ap())
        self.outT = dt("outT", [D, S], F32, kind="ExternalOutput").ap()
        self.xs = dt("xs", [D, NT], F32, kind="ExternalOutput" if dbg else "Internal").ap()
        sk = "ExternalOutput" if dbg else "Internal"
        self.hT = dt("hT", [D, NT], BF16, kind=sk).ap()
        self.oT = dt("oT", [D, NT], BF16, kind=sk).ap()
        self.mT = dt("mT", [DFF, NT], BF16, kind=sk).ap()

        with ExitStack() as es:
            self.k = Ctx(nc, es)
            sb = lambda name, shape, dty: es.enter_context(nc.sbuf_tensor(self.U() + name, shape, dty))
            self.vec = sb("vec_sb", [128, NV], F32)
            self.vecT = T()
            self.mod = sb("mod_sb", [128, DEPTH * 48 * 2], F32)
            self.modT = T()
            self.cstb = sb("cst_sb", [128, 512], BF16)
            self.cstT = T()
            self.onesm = sb("onesm", [128, 128], BF16)
            self.ones1 = sb("ones1", [128, 128], BF16)
            self.onesT = T()
            self.small = sb("small", [128, 64], F32)
            self.smallT = T()
            self.gsub = sb("gsub", [128, 256], F32)
            self.gsubT = T()
            self.build()

    def bank(self, es, dty=F32):
        shape = [128, 512] if dty == F32 else [128, 1024]
        return es.enter_context(self.nc.psum_tensor(self.U() + "bank", shape, dty)), T(True)

    def banks(self, es, n, dty=F32):
        bs = [self.bank(es, dty) for _ in range(n)]
        return [b for b, _ in bs], [t for _, t in bs]

    def U(self):
        self._uid = getattr(self, "_uid", 0) + 1
        return "t%d_" % self._uid

    def v(self, name, a=0, b=None):
        o, w = VOFF[name]
        if b is None:
            b = w
        return self.vec[:, o + a:o + b]

    def modcol(self, l, ch, s):
        i = (l * 48 + ch) * 2 + s
        return self.mod[:, i:i + 1]

    def __getattribute__(self, name):
        a = object.__getattribute__(self, name)
        if name.startswith("phase_"):
            def wrapped(*args, **kw):
                self._pc = getattr(self, "_pc", 0) + 1
                if self._pc > self.nphase:
                    return None
                return a(*args, **kw)
            return wrapped
        return a

    def build(self):
        k, nc = self.k, self.nc
        k.dma("sp", self.vec[:], self.vec_d[:, :], k.slot(), writes=[self.vecT])
        k.dma("pool", self.cstb[:], self.cst_d[:, :], k.slot(), writes=[self.cstT])
        k.op("pool", MS(self.onesm[:], 1.0 / 1024.0), writes=[self.onesT])
        k.barrier()
        self.phase_mod()
        k.barrier()
        xsrc = self.xin
        for l in self.layers:
            last = (l == DEPTH - 1)
            need_ctx = not last
            kind, j = l % 3, l // 3
            self.phase_norm(xsrc, l, 1, True)
            k.barrier()
            if kind == 0:
                self.phase_attn_diff(l, j, need_ctx)
                wo = self.a_wo[self.ja[j]]
            elif kind == 1:
                self.phase_attn_na(l, need_ctx)
                wo = self.b_wo[0]
            else:
                self.phase_attn_swa(l, need_ctx)
                wo = self.c_wo[0]
            k.barrier()
            self.phase_wo(l, wo, xsrc, need_ctx)
            xsrc = self.xs
            k.barrier()
            self.phase_norm(xsrc, l, 2, need_ctx)
            k.barrier()
            self.phase_ffnup(l, need_ctx)
            k.barrier()
            self.phase_ffndown(l, need_ctx)
            k.barrier()
        self.phase_norm(xsrc, 0, 0, False)
        k.barrier()

    def phase_mod(self):
        k, nc = self.k, self.nc
        with ExitStack() as es:
            wb = [es.enter_context(nc.sbuf_tensor(self.U() + "adw%d" % i, [128, 8, 512], F32)) for i in range(2)]
            wT = [T(), T()]
            wS = [k.slot(), k.slot()]
            scc = es.enter_context(nc.sbuf_tensor(self.U() + "scc", [128, 16], F32))
            sccT = T()
            psb, psT = self.bank(es)
            ps = psb[:, 0:96].rearrange("p (c s) -> p c s", s=2)
            k.op("act", ACT(scc[:], self.v("cc"), AF.Silu), reads=[self.vecT], writes=[sccT])
            modv = self.mod[:].rearrange("p (l c s) -> p l c s", l=DEPTH, c=48, s=2)
            jobs = [(l, nb) for l in self.layers for nb in range(12)]

            def load(i):
                l, nb = jobs[i]
                k.dma("sp", wb[i % 2][:], kp(self.ada_w[self.li[l]])[:, :, nb * 512:(nb + 1) * 512], wS[i % 2], writes=[wT[i % 2]])
            load(0)
            for i, (l, nb) in enumerate(jobs):
                if i + 1 < len(jobs):
                    load(i + 1)
                for fc in range(4):
                    ch = nb * 4 + fc
                    k.op("pe", [MM(ps[:, ch, :], wb[i % 2][:, kc, fc * 128:(fc + 1) * 128], scc[:, kc * 2:kc * 2 + 2],
                                   start=(kc == 0), stop=(kc == 7)) for kc in range(8)],
                         reads=[wT[i % 2], sccT], writes=[psT])
                if nb == 11:
                    o, _ = VOFF["adab"]
                    for s in range(2):
                        k.op("dve", TT(modv[:, l, :, s], ps[:, :, s], self.vec[:, o + l * 48:o + (l + 1) * 48], ALU.add),
                             reads=[psT, self.vecT], writes=[self.modT])
                    for s in range(2):
                        k.op("dve", STT(modv[:, l, 8:16, s], modv[:, l, 8:16, s], 1.0, self.v("nm", l * 8, l * 8 + 8), ALU.add, ALU.mult),
                             reads=[self.modT, self.vecT], writes=[self.modT])
                        k.op("dve", STT(modv[:, l, 32:40, s], modv[:, l, 32:40, s], 1.0, self.v("nf", l * 8, l * 8 + 8), ALU.add, ALU.mult),
                             reads=[self.modT, self.vecT], writes=[self.modT])
            tmp = es.enter_context(nc.sbuf_tensor(self.U() + "lamtmp", [128, 64], F32))
            acc = es.enter_context(nc.sbuf_tensor(self.U() + "lamacc", [128, 8], F32))
            tT = T()
            import math
            for j in range(2):
                l = 3 * j
                lam_init = 0.8 - 0.6 * math.exp(-0.3 * l)
                for h in range(2):
                    a0 = self.v("lam", (j * 4 + 2 * h) * 64, (j * 4 + 2 * h + 1) * 64)
                    a1 = self.v("lam", (j * 4 + 2 * h + 1) * 64, (j * 4 + 2 * h + 2) * 64)
                    k.op("dve", STT(tmp[:], a0, 1.0, a1, ALU.mult, ALU.mult, accum_out=acc[:, h:h + 1]),
                         reads=[self.vecT], writes=[tT])
                k.op("act", ACT(acc[:, 2:4], acc[:, 0:2], AF.Exp), reads=[tT], writes=[tT])
                k.op("dve", STT(self.small[:, j:j + 1], acc[:, 3:4], -lam_init, acc[:, 2:3], ALU.add, ALU.subtract),
                     reads=[tT], writes=[self.smallT])
                k.op("dve", TS(self.gsub[:, j * 128:(j + 1) * 128], self.v("subln", j * 128, (j + 1) * 128),
                               (1.0 - lam_init) * math.sqrt(128.0)), reads=[self.vecT], writes=[self.gsubT])
            k.op("act", ACT(self.small[:, 8:24], self.v("sink"), AF.Exp), reads=[self.vecT, self.smallT], writes=[self.smallT])

    def phase_norm(self, xsrc, l, which, with_ctx):
        k, nc = self.k, self.nc
        blocks = BLOCKS if with_ctx else BLOCKS[:8]
        odt = F32 if which == 0 else BF16
        with ExitStack() as es:
            sb = lambda name, shape, dty: es.enter_context(nc.sbuf_tensor(self.U() + name, shape, dty))
            xb = [sb("nx%d" % i, [128, 8, 512], F32) for i in range(2)]
            xT_ = [T(), T()]
            xS = [k.slot(), k.slot()]
            sq = [sb("nsq%d" % i, [128, 8, 512], BF16) for i in range(2)]
            sqT = [T(), T()]
            rs = [sb("nrs%d" % i, [128, 512], F32) for i in range(2)]
            rsT = [T(), T()]
            tmp = [sb("ntmp%d" % i, [128, 8, 512], F32) for i in range(2)]
            tmpT = [[T() for _ in range(8)] for _ in range(2)]
            hb = [sb("nh%d" % i, [128, 8, 512], odt) for i in range(2)]
            hT_ = [T(), T()]
            hS = [k.slot(), k.slot()]
            ss, ssT = self.banks(es, 2)

            def load(i):
                t0, N, s = blocks[i]
                k.dma("sp", xb[i % 2][:, :, :N], kp(xsrc)[:, :, t0:t0 + N], xS[i % 2], writes=[xT_[i % 2]])
            load(0)
            for i, (t0, N, s) in enumerate(blocks):
                b = i % 2
                if i + 1 < len(blocks):
                    load(i + 1)
                k.op("dve", TT(sq[b][:, :, :N], xb[b][:, :, :N], xb[b][:, :, :N], ALU.mult), reads=[xT_[b]], writes=[sqT[b]])
                k.op("pe", [MM(ss[b][:, :N], self.onesm[:], sq[b][:, kc, :N], start=(kc == 0), stop=(kc == 7)) for kc in range(8)],
                     reads=[sqT[b], self.onesT], writes=[ssT[b]])
                k.op("act", ACT(rs[b][:, :N], ss[b][:, :N], AF.Ln, bias=EPS, scale=1.0), reads=[ssT[b]], writes=[rsT[b]])
                k.op("act", ACT(rs[b][:, :N], rs[b][:, :N], AF.Exp, scale=-0.5), reads=[rsT[b]], writes=[rsT[b]])
                for kc in range(8):
                    k.op("dve" if kc % 2 == 0 else "pool", TT(tmp[b][:, kc, :N], xb[b][:, kc, :N], rs[b][:, :N], ALU.mult),
                         reads=[xT_[b], rsT[b]], writes=[tmpT[b][kc]])
                for kc in range(8):
                    if which == 0:
                        fn = ACT(hb[b][:, kc, :N], tmp[b][:, kc, :N], AF.Identity, scale=self.v("no", kc, kc + 1))
                    else:
                        ach = (8 if which == 1 else 32) + kc
                        sch = (0 if which == 1 else 24) + kc
                        fn = ACT(hb[b][:, kc, :N], tmp[b][:, kc, :N], AF.Identity, scale=self.modcol(l, ach, s), bias=self.modcol(l, sch, s))
                    k.op("act", fn, reads=[tmpT[b][kc], self.modT, self.vecT], writes=[hT_[b]])
                dst = self.outT if which == 0 else self.hT
                k.dma("sp", kp(dst)[:, :, t0:t0 + N], hb[b][:, :, :N], hS[b], reads=[hT_[b]])

    def phase_wo(self, l, wo_d, xsrc, with_ctx):
        k, nc = self.k, self.nc
        blocks = BLOCKS if with_ctx else BLOCKS[:8]
        with ExitStack() as es:
            sb = lambda name, shape, dty: es.enter_context(nc.sbuf_tensor(self.U() + name, shape, dty))
            wo = sb("wo", [128, 8, 1024], BF16)
            woT = [T() for _ in range(8)]
            for kc in range(8):
                k.dma("pool", wo[:, kc, :], wo_d[kc * 128:(kc + 1) * 128, :], k.slot(), writes=[woT[kc]])
            ob = [sb("wob%d" % i, [128, 8, 512], BF16) for i in range(2)]
            obT = [T(), T()]
            obS = [k.slot(), k.slot()]
            xb = [sb("wxb%d" % i, [128, 8, 512], F32) for i in range(2)]
            xbT = [T(), T()]
            xbS = [k.slot(), k.slot()]
            xn = [sb("wxn%d" % i, [128, 8, 512], F32) for i in range(2)]
            xnT = [T(), T()]
            xnS = [k.slot(), k.slot()]
            ps, psT = self.banks(es, 4)

            def load(i):
                t0, N, s = blocks[i]
                k.dma("sp", ob[i % 2][:, :, :N], kp(self.oT)[:, :, t0:t0 + N], obS[i % 2], writes=[obT[i % 2]])
                k.dma("sp", xb[i % 2][:, :, :N], kp(xsrc)[:, :, t0:t0 + N], xbS[i % 2], writes=[xbT[i % 2]])
            load(0)
            cnt = 0
            for i, (t0, N, s) in enumerate(blocks):
                b = i % 2
                if i + 1 < len(blocks):
                    load(i + 1)
                for m in range(8):
                    p = cnt % 4
                    cnt += 1
                    k.op("pe", [MM(ps[p][:, :N], wo[:, kc, m * 128:(m + 1) * 128], ob[b][:, kc, :N], start=(kc == 0), stop=(kc == 7))
                                for kc in range(8)], reads=woT + [obT[b]], writes=[psT[p]])
                    k.op("dve", STT(xn[b][:, m, :N], ps[p][:, :N], self.modcol(l, 16 + m, s), xb[b][:, m, :N], ALU.mult, ALU.add),
                         reads=[psT[p], xbT[b], self.modT], writes=[xnT[b]])
                k.dma("sp", kp(self.xs)[:, :, t0:t0 + N], xn[b][:, :, :N], xnS[b], reads=[xnT[b]])

    def phase_ffnup(self, l, with_ctx):
        k, nc = self.k, self.nc
        blocks = BLOCKS if with_ctx else BLOCKS[:8]
        ntok = NT if with_ctx else S
        UW = NT + 4
        with ExitStack() as es:
            sb = lambda name, shape, dty: es.enter_context(nc.sbuf_tensor(self.U() + name, shape, dty))
            H = sb("fH", [128, 8, NT], BF16)
            HT = [T() for _ in range(8)]
            for kc in range(8):
                k.dma("sp", H[:, kc, :ntok], self.hT[kc * 128:(kc + 1) * 128, :ntok], k.slot(), writes=[HT[kc]])
            wa = [sb("fwa%d" % i, [128, 8, 256], BF16) for i in range(2)]
            waT = [(T(), T()), (T(), T())]
            waS = [(k.slot(), k.slot()) for _ in range(2)]
            uas = [sb("fua%d" % i, [128, UW], F32) for i in range(2)]
            ugs = [sb("fug%d" % i, [128, UW], F32) for i in range(2)]
            uaTs, ugTs = [T(), T()], [T(), T()]
            aa = sb("faa", [128, UW], F32)
            ag = sb("fag", [128, UW], F32)
            aaT, agT = T(), T()
            mb1 = sb("fmb", [128, NT], BF16)
            mb = [mb1, mb1]
            mbT1 = T()
            mbT = [mbT1, mbT1]
            mbS1 = k.slot()
            mbS = [mbS1, mbS1]
            ps, psT = self.banks(es, 4)
            for i_ in range(2):
                k.op("pool", MS(uas[i_][:], 0.0), writes=[uaTs[i_]])
                k.op("pool", MS(ugs[i_][:], 0.0), writes=[ugTs[i_]])
            up = kp(self.ffn_up[self.li[l]])
            cwo, _ = VOFF["cw"]
            cbo, _ = VOFF["cb"]

            def load(i):
                k.dma("pool", wa[i % 2][:, :, 0:128], up[:, :, i * 128:(i + 1) * 128], waS[i % 2][0], writes=[waT[i % 2][0]])
                k.dma("pool", wa[i % 2][:, :, 128:256], up[:, :, DFF + i * 128:DFF + (i + 1) * 128], waS[i % 2][1], writes=[waT[i % 2][1]])

            def ucol(t0):
                return t0 + 1 if t0 < S else t0 + 3
            load(0)
            cnt = 0
            hi = ucol(ntok - 1) + 1

            def tail(i):
                b = i % 2
                k.op("act", ACT(ag[:, 1:hi], ag[:, 1:hi], AF.Silu), reads=[agT], writes=[agT])
                k.op("pool", TT(mb[b][:, 0:S], ag[:, 1:S + 1], aa[:, 1:S + 1], ALU.mult), reads=[agT, aaT], writes=[mbT[b]])
                if with_ctx:
                    k.op("pool", TT(mb[b][:, S:NT], ag[:, S + 3:NT + 3], aa[:, S + 3:NT + 3], ALU.mult), reads=[agT, aaT], writes=[mbT[b]])
                k.dma("sp", self.mT[i * 128:(i + 1) * 128, :ntok], mb[b][:, :ntok], mbS[b], reads=[mbT[b]])
            for i in range(22):
                b = i % 2
                ua, ug, uaT, ugT = uas[b], ugs[b], uaTs[b], ugTs[b]
                if i + 1 < 22:
                    load(i + 1)
                for (t0, N, s) in blocks:
                    for half, (u, uT) in enumerate(((ua, uaT), (ug, ugT))):
                        p = cnt % 4
                        cnt += 1
                        k.op("pe", [MM(ps[p][:, :N], wa[b][:, kc, half * 128:(half + 1) * 128], H[:, kc, t0:t0 + N],
                                       start=(kc == 0), stop=(kc == 7)) for kc in range(8)],
                             reads=[waT[b][half]] + HT, writes=[psT[p]])
                        c0 = ucol(t0)
                        k.op("act", ACT(u[:, c0:c0 + N], ps[p][:, :N], AF.Copy), reads=[psT[p]], writes=[uT])
                if i > 0:
                    tail(i - 1)
                for half, (u, uT, a, aT) in enumerate(((ua, uaT, aa, aaT), (ug, ugT, ag, agT))):
                    ch = half * 22 + i
                    w = [self.vec[:, cwo + (l * 3 + tap) * 44 + ch: cwo + (l * 3 + tap) * 44 + ch + 1] for tap in range(3)]
                    bia = self.vec[:, cbo + l * 44 + ch: cbo + l * 44 + ch + 1]
                    k.op("dve", TS(a[:, 1:hi], u[:, 1:hi], w[1], bia, ALU.mult, ALU.add), reads=[uT, self.vecT], writes=[aT])
                    k.op("dve", STT(a[:, 1:hi], u[:, 0:hi - 1], w[0], a[:, 1:hi], ALU.mult, ALU.add), reads=[uT, aT, self.vecT], writes=[aT])
                    k.op("dve", STT(a[:, 1:hi], u[:, 2:hi + 1], w[2], a[:, 1:hi], ALU.mult, ALU.add), reads=[uT, aT, self.vecT], writes=[aT])
            tail(21)

    def phase_ffndown(self, l, with_ctx):
        k, nc = self.k, self.nc
        blocks = BLOCKS if with_ctx else BLOCKS[:8]
        with ExitStack() as es:
            sb = lambda name, shape, dty: es.enter_context(nc.sbuf_tensor(self.U() + name, shape, dty))
            wd = sb("dwd", [128, 22, 1024], BF16)
            wdT = [T() for _ in range(22)]
            for kc in range(22):
                k.dma("pool", wd[:, kc, :], self.ffn_down[self.li[l]][kc * 128:(kc + 1) * 128, :], k.slot(), writes=[wdT[kc]])
            mb = [sb("dmb%d" % i, [128, 22, 512], BF16) for i in range(2)]
            mbT = [T(), T()]
            mbS = [k.slot(), k.slot()]
            xb = [sb("dxb%d" % i, [128, 8, 512], F32) for i in range(2)]
            xbT = [T(), T()]
            xbS = [k.slot(), k.slot()]
            xn = [sb("dxn%d" % i, [128, 8, 512], F32) for i in range(2)]
            xnT = [T(), T()]
            xnS = [k.slot(), k.slot()]
            ps, psT = self.banks(es, 4)

            def load(i):
                t0, N, s = blocks[i]
                k.dma("sp", mb[i % 2][:, :, :N], kp(self.mT)[:, :, t0:t0 + N], mbS[i % 2], writes=[mbT[i % 2]])
                k.dma("sp", xb[i % 2][:, :, :N], kp(self.xs)[:, :, t0:t0 + N], xbS[i % 2], writes=[xbT[i % 2]])
            load(0)
            cnt = 0
            for i, (t0, N, s) in enumerate(blocks):
                b = i % 2
                if i + 1 < len(blocks):
                    load(i + 1)
                for m in range(8):
                    p = cnt % 4
                    cnt += 1
                    k.op("pe", [MM(ps[p][:, :N], wd[:, kc, m * 128:(m + 1) * 128], mb[b][:, kc, :N], start=(kc == 0), stop=(kc == 21))
                                for kc in range(22)], reads=wdT + [mbT[b]], writes=[psT[p]])
                    k.op("dve", STT(xn[b][:, m, :N], ps[p][:, :N], self.modcol(l, 40 + m, s), xb[b][:, m, :N], ALU.mult, ALU.add),
                         reads=[psT[p], xbT[b], self.modT], writes=[xnT[b]])
                k.dma("sp", kp(self.xs)[:, :, t0:t0 + N], xn[b][:, :, :N], xnS[b], reads=[xnT[b]])

    def load_H(self, es):
        k, nc = self.k, self.nc
        H = es.enter_context(nc.sbuf_tensor(self.U() + "aH", [128, 8, NT], BF16))
        HT = [T() for _ in range(8)]
        for kc in range(8):
            k.dma("sp", H[:, kc, :], self.hT[kc * 128:(kc + 1) * 128, :], k.slot(), writes=[HT[kc]])
        return H, HT

    def proj_fm(self, H, HT, w, wT, wcol, dst, dstT, pq, pqT, rope=None):
        k = self.k
        for bi, (t0, N, s) in enumerate(BLOCKS):
            p = self._pq
            self._pq ^= 1
            k.op("pe", [MM(pq[p][:, :N], w[:, kc, wcol:wcol + 128], H[:, kc, t0:t0 + N], start=(kc == 0), stop=(kc == 7))
                        for kc in range(8)], reads=[wT] + HT, writes=[pqT[p]])
            if rope is None or s == 1:
                k.op("act", ACT(dst[:, t0:t0 + N], pq[p][:, :N], AF.Copy), reads=[pqT[p]], writes=[dstT])
            else:
                cos, sin, cT, qraw, qrawT, pr, prT, t1, t1T, t2, t2T = rope
                k.op("act", ACT(qraw[:, :N], pq[p][:, :N], AF.Copy), reads=[pqT[p]], writes=[qrawT])
                k.op("pe", MM(pr[:, :N], self.cstb[:, 0:128], qraw[:, :N]), reads=[qrawT, self.cstT], writes=[prT])
                k.op("dve", TT(t1[:, :N], pq[p][:, :N], cos[:, t0:t0 + N], ALU.mult), reads=[pqT[p], cT], writes=[t1T])
                k.op("dve", TT(t2[:, :N], pr[:, :N], sin[:, t0:t0 + N], ALU.mult), reads=[prT, cT], writes=[t2T])
                k.op("pool", TT(dst[:, t0:t0 + N], t1[:, :N], t2[:, :N], ALU.add), reads=[t1T, t2T], writes=[dstT])

    def phase_attn_diff(self, l, j, need_ctx):
        k, nc = self.k, self.nc
        self._pq = 0
        with ExitStack() as es:
            sb = lambda name, shape, dty: es.enter_context(nc.sbuf_tensor(self.U() + name, shape, dty))
            pst = lambda name, shape, dty: es.enter_context(nc.psum_tensor(self.U() + name, shape, dty))
            H, HT = self.load_H(es)
            cos = sb("acos", [128, S], F32)
            sin = sb("asin", [128, S], F32)
            cT = T()
            k.dma("sp", cos[:], self.cos_d[:, :], k.slot(), writes=[cT])
            k.dma("sp", sin[:], self.sin_d[:, :], k.slot(), writes=[cT])
            w = [sb("aw%d" % i, [128, 8, 384], BF16) for i in range(2)]
            wT = [[T(), T(), T()], [T(), T(), T()]]
            wS = [[k.slot() for _ in range(3)] for _ in range(2)]
            qT = sb("aq", [128, NT], BF16)
            kT = sb("ak", [128, NT], BF16)
            V = sb("av", [128, 34, 132], BF16)
            qT_, kT_, VT = T(), T(), T()
            qraw = sb("aqraw", [128, 512], BF16)
            t1 = sb("at1", [128, 512], F32)
            t2 = sb("at2", [128, 512], F32)
            qrawT, t1T, t2T = T(), T(), T()
            pT = [sb("apT%d" % i, [128, 512], BF16) for i in range(3)]
            pT_ = [T() for _ in range(3)]
            OTc = [sb("aOT%d" % i, [128, NT], BF16) for i in range(2)]
            OTcT = [T(), T()]
            OTcS = [k.slot(), k.slot()]
            o0n = sb("ao0n", [128, 4, 128], F32)
            o0nT = [T() for _ in range(4)]
            o32 = sb("ao32", [128, 4, 128], F32)
            o32T = [T() for _ in range(4)]
            junk = sb("ajunk", [128, 128], F32)
            on = sb("aon", [128, 4, 128], BF16)
            onT = T()
            sc = sb("asc", [128, 16], F32)
            sc2 = sb("asc2", [128, 8], F32)
            sc2T = T()
            finT = [T() for _ in range(4)]
            pq, pqT = self.banks(es, 2)
            po, poT = self.banks(es, 4)
            pr, prT = self.bank(es)
            ptrb, ptrT = self.bank(es, BF16)
            ptr = ptrb[:, 0:512]
            k.op("pool", MS(V[:, :, 128:129], 1.0), writes=[VT])
            wq = kp(self.a_wqkv[self.ja[j]])
            neg_lam = self.small[:, j:j + 1]
            G = self.gsub[:, j * 128:(j + 1) * 128]
            rope = (cos, sin, cT, qraw, qrawT, pr, prT, t1, t1T, t2, t2T)

            def loadw(c):
                for part in range(3):
                    k.dma("pool", w[c % 2][:, :, part * 128:(part + 1) * 128], wq[:, :, part * D + c * 128: part * D + (c + 1) * 128],
                          wS[c % 2][part], writes=[wT[c % 2][part]])
            loadw(0)
            sidx = 0
            pidx = 0
            for c in range(8):
                if c + 1 < 8:
                    loadw(c + 1)
                wc, wcT = w[c % 2], wT[c % 2]
                self.proj_fm(H, HT, wc, wcT[0], 0, qT, qT_, pq, pqT, rope)
                self.proj_fm(H, HT, wc, wcT[1], 128, kT, kT_, pq, pqT, rope)
                for g0 in range(0, 34, 4):
                    g1 = min(34, g0 + 4)
                    p = self._pq
                    self._pq ^= 1
                    fns = []
                    for tt in range(g0, g1):
                        for kc in range(8):
                            fns.append(MM(pq[p][:, (tt - g0) * 128:(tt - g0 + 1) * 128], H[:, kc, tt * 128:(tt + 1) * 128],
                                          wc[:, kc, 256:384], start=(kc == 0), stop=(kc == 7)))
                    k.op("pe", fns, reads=[wcT[2]] + HT, writes=[pqT[p]])
                    k.op("act", ACT(V[:, g0:g1, 0:128], pq[p][:, 0:(g1 - g0) * 128].rearrange("p (a b) -> p a b", b=128), AF.Copy),
                         reads=[pqT[p]], writes=[VT])
                OT, OTT = OTc[c % 2], OTcT[c % 2]
                qblocks = [(qb * 512, 512, list(range(34))) for qb in range(8)]
                if need_ctx:
                    qblocks.append((S, C, [32, 33]))
                tasks = []
                for (q0, QN, ktiles) in qblocks:
                    for jm in range(2):
                        for ki, kt in enumerate(ktiles):
                            tasks.append((q0, QN, QN // 128, jm, ki, kt, len(ktiles)))

                def stageA(i):
                    q0, QN, nqs, jm, ki, kt, nk = tasks[i]
                    pl = slice(jm * 64, (jm + 1) * 64)
                    sp_, pb = i % 2, i % 3
                    k.op("pe", MM(pq[sp_][:, :QN], kT[pl, kt * 128:(kt + 1) * 128], qT[pl, q0:q0 + QN]),
                         reads=[kT_, qT_], writes=[pqT[sp_]])
                    k.op("act", ACT(pT[pb][:, :QN], pq[sp_][:, :QN], AF.Exp, scale=0.125), reads=[pqT[sp_]], writes=[pT_[pb]])

                def stageB(i):
                    q0, QN, nqs, jm, ki, kt, nk = tasks[i]
                    pb = i % 3
                    k.op("pe", [MM(po[qs][:, 0:129], pT[pb][:, qs * 128:(qs + 1) * 128], V[:, kt, 0:129],
                                   start=(ki == 0), stop=(ki == nk - 1)) for qs in range(nqs)],
                         reads=[pT_[pb], VT], writes=poT[:nqs])
                    if ki != nk - 1:
                        return
                    if jm == 0:
                        for qs in range(nqs):
                            k.op("dve", RCP(sc[:, qs:qs + 1], po[qs][:, 128:129]), reads=[poT[qs]], writes=[finT[qs]])
                            k.op("dve", TS(o0n[:, qs, :], po[qs][:, 0:128], sc[:, qs:qs + 1]), reads=[poT[qs], finT[qs]], writes=[o0nT[qs]])
                        return
                    for qs in range(nqs):
                        k.op("dve", RCP(sc[:, 4 + qs:5 + qs], po[qs][:, 128:129]), reads=[poT[qs]], writes=[finT[qs]])
                    for qs in range(nqs):
                        k.op("dve", TT(sc[:, 8 + qs:9 + qs], sc[:, 4 + qs:5 + qs], neg_lam, ALU.mult), reads=[finT[qs], self.smallT], writes=[finT[qs]])
                    for qs in range(nqs):
                        k.op("dve", STT(o32[:, qs, :], po[qs][:, 0:128], sc[:, 8 + qs:9 + qs], o0n[:, qs, :], ALU.mult, ALU.add),
                             reads=[poT[qs], finT[qs], o0nT[qs]], writes=[o32T[qs]])
                    for qs in range(nqs):
                        k.op("dve", STT(junk[:], o32[:, qs, :], 1.0, o32[:, qs, :], ALU.mult, ALU.mult, accum_out=sc[:, 12 + qs:13 + qs]),
                             reads=[o32T[qs]], writes=[finT[qs]])
                    k.op("act", ACT(sc2[:, 0:nqs], sc[:, 12:12 + nqs], AF.Ln, bias=128.0 * EPS, scale=1.0), reads=finT[:nqs], writes=[sc2T])
                    k.op("act", ACT(sc2[:, 4:4 + nqs], sc2[:, 0:nqs], AF.Exp, scale=-0.5), reads=[sc2T], writes=[sc2T])
                    for qs in range(nqs):
                        k.op("dve", STT(on[:, qs, :], o32[:, qs, :], sc2[:, 4 + qs:5 + qs], G, ALU.mult, ALU.mult),
                             reads=[o32T[qs], sc2T, self.gsubT], writes=[onT])
                    k.op("pe", [TR(ptr[:, qs * 128:(qs + 1) * 128], on[:, qs, :], self.cstb[:, 128:256]) for qs in range(nqs)],
                         reads=[onT, self.cstT], writes=[ptrT])
                    k.op("dve", CP(OT[:, q0:q0 + nqs * 128], ptr[:, 0:nqs * 128]), reads=[ptrT], writes=[OTT])

                stageA(0)
                for i in range(len(tasks)):
                    if i + 1 < len(tasks):
                        stageA(i + 1)
                    stageB(i)
                ntok = NT if need_ctx else S
                k.dma("sp", self.oT[c * 128:(c + 1) * 128, :ntok], OT[:, :ntok], OTcS[c % 2], reads=[OTT])

    def ctx_dense(self, es_objs, qT, kT, qT_, kT_, pl, vrhs, VT, dv, sink, dst, dstT):
        k = self.k
        pq, pqT, pc, pcT, po2, po2T, sc, scT = es_objs
        for tt in range(2):
            k.op("pe", MM(pq[tt][:, :C], kT[pl, S + tt * 128:S + (tt + 1) * 128], qT[pl, S:NT]), reads=[kT_, qT_], writes=[pqT[tt]])
            k.op("act", ACT(pc[tt][:, :C], pq[tt][:, :C], AF.Exp, scale=0.125), reads=[pqT[tt]], writes=[pcT[tt]])
        for qs in range(2):
            k.op("pe", [MM(po2[:, qs * 128:qs * 128 + dv + 1], pc[tt][:, qs * 128:(qs + 1) * 128], vrhs(tt), start=(tt == 0), stop=(tt == 1))
                        for tt in range(2)], reads=[pcT[0], pcT[1], VT], writes=[po2T])
        for qs in range(2):
            den = po2[:, qs * 128 + dv:qs * 128 + dv + 1]
            if sink is not None:
                k.op("dve", TS(sc[:, 0:1], den, sink, op0=ALU.add), reads=[po2T, self.smallT], writes=[scT])
                k.op("dve", RCP(sc[:, 1:2], sc[:, 0:1]), reads=[scT], writes=[scT])
            else:
                k.op("dve", RCP(sc[:, 1:2], den), reads=[po2T], writes=[scT])
            k.op("dve", TS(dst(qs), po2[:, qs * 128:qs * 128 + dv], sc[:, 1:2]), reads=[po2T, scT], writes=[dstT])

    def phase_attn_na(self, l, need_ctx):
        k, nc = self.k, self.nc
        self._pq = 0
        with ExitStack() as es:
            sb = lambda name, shape, dty: es.enter_context(nc.sbuf_tensor(self.U() + name, shape, dty))
            pst = lambda name, shape, dty: es.enter_context(nc.psum_tensor(self.U() + name, shape, dty))
            H, HT = self.load_H(es)
            w = [sb("bw%d" % i, [128, 8, 384], BF16) for i in range(2)]
            wT = [[T(), T(), T()], [T(), T(), T()]]
            wS = [[k.slot() for _ in range(3)] for _ in range(2)]
            NB = [sb("bnb%d" % i, [64, 2, 960], F32) for i in range(2)]
            NBT = [[T(), T()], [T(), T()]]
            NBS = [[k.slot(), k.slot()], [k.slot(), k.slot()]]
            qT = sb("bq", [128, NT], BF16)
            kT = sb("bk", [128, NT], BF16)
            Vn = sb("bvn", [64, 64, 2, 66], BF16)
            Vc = sb("bvc", [128, 2, 2, 66], BF16)
            qT_, kT_, VT = T(), T(), T()
            ssb = [sb("bss%d" % i, [64, 512], F32) for i in range(2)]
            ssbT = [T(), T()]
            P = [sb("bP%d" % i, [64, 512], BF16) for i in range(2)]
            PT = [T(), T()]
            Pc = [sb("bPc%d" % i, [128, 128], BF16) for i in range(2)]
            PcT = [T(), T()]
            otok = [sb("bot%d" % i, [64, 4, 128], BF16) for i in range(2)]
            otokT = [T(), T()]
            oc = sb("boc", [128, 2, 128], BF16)
            ocT = T()
            pc = [sb("bpc%d" % i, [128, C], BF16) for i in range(2)]
            pcT = [T(), T()]
            sc = sb("bsc", [128, 8], F32)
            scT = T()
            rc = sb("brc", [64, 8], F32)
            rcT = T()
            OTc = [sb("bOT%d" % i, [128, NT], BF16) for i in range(2)]
            OTcT = [T(), T()]
            OTcS = [k.slot(), k.slot()]
            pq, pqT = self.banks(es, 2)
            pob_, poT = self.banks(es, 2)
            po = [b[0:64, :].rearrange("p (a b) -> p a b", b=128) for b in pob_]
            po2b, po2T = self.bank(es)
            po2 = po2b[:, 0:256]
            pscb, pscT = self.banks(es, 2)
            psc = [b[:, 0:128] for b in pscb]
            ptrb, ptrT = self.bank(es, BF16)
            ptr = ptrb[:, 0:256]
            k.op("pool", MS(Vn[:, :, :, 64:65], 1.0), writes=[VT])
            k.op("pool", MS(Vc[:, :, :, 64:65], 1.0), writes=[VT])
            wq = kp(self.b_wqkv[0])
            ident = self.cstb[:, 128:256]

            def loadw(c):
                for part in range(3):
                    k.dma("pool", w[c % 2][:, :, part * 128:(part + 1) * 128], wq[:, :, part * D + c * 128: part * D + (c + 1) * 128],
                          wS[c % 2][part], writes=[wT[c % 2][part]])
                for jh in range(2):
                    k.dma("sp", NB[c % 2][:, jh, :], self.natab_d[2 * c + jh], NBS[c % 2][jh], writes=[NBT[c % 2][jh]])
            loadw(0)
            sidx = 0
            for c in range(8):
                if c + 1 < 8:
                    loadw(c + 1)
                wc, wcT = w[c % 2], wT[c % 2]
                self.proj_fm(H, HT, wc, wcT[0], 0, qT, qT_, pq, pqT, None)
                self.proj_fm(H, HT, wc, wcT[1], 128, kT, kT_, pq, pqT, None)
                for r0 in range(0, 64, 4):
                    p = self._pq
                    self._pq ^= 1
                    fns = []
                    for rr in range(4):
                        for kc in range(8):
                            fns.append(MM(pq[p][0:64, rr * 128:(rr + 1) * 128], H[:, kc, (r0 + rr) * 64:(r0 + rr + 1) * 64],
                                          wc[:, kc, 256:384], start=(kc == 0), stop=(kc == 7)))
                    k.op("pe", fns, reads=[wcT[2]] + HT, writes=[pqT[p]])
                    k.op("act", ACT(Vn[:, r0:r0 + 4, :, 0:64], pq[p][0:64, :].rearrange("p (a b c) -> p a b c", a=4, b=2), AF.Copy),
                         reads=[pqT[p]], writes=[VT])
                p = self._pq
                self._pq ^= 1
                fns = []
                for tt in range(2):
                    for kc in range(8):
                        fns.append(MM(pq[p][:, tt * 128:(tt + 1) * 128], H[:, kc, S + tt * 128:S + (tt + 1) * 128],
                                      wc[:, kc, 256:384], start=(kc == 0), stop=(kc == 7)))
                k.op("pe", fns, reads=[wcT[2]] + HT, writes=[pqT[p]])
                k.op("act", ACT(Vc[:, :, :, 0:64], pq[p][:, 0:256].rearrange("p (a b c) -> p a b c", a=2, b=2), AF.Copy),
                     reads=[pqT[p]], writes=[VT])
                OT, OTT = OTc[c % 2], OTcT[c % 2]
                nbt = NB[c % 2]
                tasks = [(rg, jh, rr) for rg in range(16) for jh in range(2) for rr in range(4)]

                def geo(i):
                    rg, jh, rr = tasks[i]
                    r = rg * 4 + rr
                    rs_ = min(max(r - 4, 0), 56)
                    return rg, jh, rr, r, rs_, rs_ - r + 7, slice(jh * 64, (jh + 1) * 64), i % 2

                def stageA(i):
                    rg, jh, rr, r, rs_, dy0, pl, sp_ = geo(i)
                    k.op("pe", [MM(pq[sp_][0:64, a * 64:(a + 1) * 64], kT[pl, (rs_ + a) * 64:(rs_ + a + 1) * 64], qT[pl, r * 64:(r + 1) * 64])
                                for a in range(8)], reads=[kT_, qT_], writes=[pqT[sp_]])
                    k.op("pe", [MM(psc[sp_][:, tt * 64:(tt + 1) * 64], kT[pl, S + tt * 128:S + (tt + 1) * 128], qT[pl, r * 64:(r + 1) * 64])
                                for tt in range(2)], reads=[kT_, qT_], writes=[pscT[sp_]])
                    k.op("dve", STT(ssb[sp_][:], pq[sp_][0:64, :], 0.125, nbt[:, jh, dy0 * 64:(dy0 + 8) * 64], ALU.mult, ALU.add),
                         reads=[pqT[sp_], NBT[c % 2][jh]], writes=[ssbT[sp_]])
                    k.op("act", ACT(P[sp_][:], ssb[sp_][:], AF.Exp), reads=[ssbT[sp_]], writes=[PT[sp_]])
                    k.op("act", ACT(Pc[sp_][:], psc[sp_][:], AF.Exp, scale=0.125), reads=[pscT[sp_]], writes=[PcT[sp_]])

                def stageB(i):
                    rg, jh, rr, r, rs_, dy0, pl, sp_ = geo(i)
                    ob, pob = rg % 2, jh
                    fns = [MM(po[pob][:, rr, 0:65], P[sp_][:, a * 64:(a + 1) * 64], Vn[:, rs_ + a, jh, 0:65], start=(a == 0), stop=False)
                           for a in range(8)]
                    fns += [MM(po[pob][:, rr, 0:65], Pc[sp_][:, tt * 64:(tt + 1) * 64], Vc[:, tt, jh, 0:65], start=False, stop=(tt == 1))
                            for tt in range(2)]
                    k.op("pe", fns, reads=[PT[sp_], PcT[sp_], VT], writes=[poT[pob]])
                    if rr != 3:
                        return
                    k.op("dve", RCP(rc[:, jh * 4:jh * 4 + 4], po[pob][:, :, 64]), reads=[poT[pob]], writes=[rcT])
                    for r4 in range(4):
                        k.op("dve", TS(otok[ob][:, r4, jh * 64:(jh + 1) * 64], po[pob][:, r4, 0:64], rc[:, jh * 4 + r4:jh * 4 + r4 + 1]),
                             reads=[poT[pob], rcT], writes=[otokT[ob]])
                    if jh != 1:
                        return
                    k.op("pe", [TR(ptr[:, r4 * 64:(r4 + 1) * 64], otok[ob][:, r4, :], ident[0:64, 0:64]) for r4 in range(4)],
                         reads=[otokT[ob], self.cstT], writes=[ptrT])
                    k.op("dve", CP(OT[:, rg * 256:(rg + 1) * 256], ptr[:, 0:256]), reads=[ptrT], writes=[OTT])

                stageA(0)
                for i in range(len(tasks)):
                    if i + 1 < len(tasks):
                        stageA(i + 1)
                    stageB(i)
                if need_ctx:
                    for jh in range(2):
                        pl = slice(jh * 64, (jh + 1) * 64)
                        self.ctx_dense((pq, pqT, pc, pcT, po2, po2T, sc, scT), qT, kT, qT_, kT_, pl,
                                       (lambda tt, jh=jh: Vc[:, tt, jh, 0:65]), VT, 64, None,
                                       (lambda qs, jh=jh: oc[:, qs, jh * 64:(jh + 1) * 64]), ocT)
                    for qs in range(2):
                        k.op("pe", TR(ptr[:, 0:128], oc[:, qs, :], ident), reads=[ocT, self.cstT], writes=[ptrT])
                        k.op("dve", CP(OT[:, S + qs * 128:S + (qs + 1) * 128], ptr[:, 0:128]), reads=[ptrT], writes=[OTT])
                ntok = NT if need_ctx else S
                k.dma("sp", self.oT[c * 128:(c + 1) * 128, :ntok], OT[:, :ntok], OTcS[c % 2], reads=[OTT])

    def phase_attn_swa(self, l, need_ctx):
        k, nc = self.k, self.nc
        self._pq = 0
        with ExitStack() as es:
            sb = lambda name, shape, dty: es.enter_context(nc.sbuf_tensor(self.U() + name, shape, dty))
            pst = lambda name, shape, dty: es.enter_context(nc.psum_tensor(self.U() + name, shape, dty))
            H, HT = self.load_H(es)
            cos = sb("ccos", [128, S], F32)
            sin = sb("csin", [128, S], F32)
            cT = T()
            k.dma("sp", cos[:], self.cos_d[:, :], k.slot(), writes=[cT])
            k.dma("sp", sin[:], self.sin_d[:, :], k.slot(), writes=[cT])
            w = [sb("cw%d" % i, [128, 8, 384], BF16) for i in range(2)]
            wT = [[T(), T(), T(), T()], [T(), T(), T(), T()]]
            wS = [[k.slot() for _ in range(4)] for _ in range(2)]
            qT = sb("cq", [128, NT], BF16)
            kT = sb("ck", [128, NT], BF16)
            V = sb("cv", [128, 34, 66], BF16)
            qT_, kT_, VT = T(), T(), T()
            qraw = sb("cqraw", [128, 512], BF16)
            t1 = sb("ct1", [128, 512], F32)
            t2 = sb("ct2", [128, 512], F32)
            qrawT, t1T, t2T = T(), T(), T()
            Pl = [sb("cPl%d" % i, [128, 3, 128], BF16) for i in range(2)]
            PlT = [T(), T()]
            Pcx = [sb("cPc%d" % i, [128, 2, 128], BF16) for i in range(2)]
            PcxT = [T(), T()]
            on = [sb("con%d" % i, [128, 128], BF16) for i in range(2)]
            onT = [T(), T()]
            oc = sb("coc", [128, 2, 128], BF16)
            ocT = T()
            pc = [sb("cpc%d" % i, [128, C], BF16) for i in range(2)]
            pcT = [T(), T()]
            sc = sb("csc", [128, 8], F32)
            scT = T()
            OTc = [sb("cOT%d" % i, [128, NT], BF16) for i in range(2)]
            OTcT = [T(), T()]
            OTcS = [k.slot(), k.slot()]
            pq, pqT = self.banks(es, 2)
            pr, prT = self.bank(es)
            pcxb, pcxT = self.banks(es, 2)
            pcx = [b[:, 0:256] for b in pcxb]
            pob_, poT = self.banks(es, 2)
            po = [b[:, 0:256].rearrange("p (a b) -> p a b", b=128) for b in pob_]
            ptrb, ptrT = self.bank(es, BF16)
            ptr = ptrb[:, 0:128]
            k.op("pool", MS(V[:, :, 64:65], 1.0), writes=[VT])
            wq = kp(self.c_wqkv[0])
            ident = self.cstb[:, 128:256]
            mprev = self.cstb[:, 256:384]
            mnext = self.cstb[:, 384:512]
            rope = (cos, sin, cT, qraw, qrawT, pr, prT, t1, t1T, t2, t2T)

            def loadw(c):
                kv = c // 2
                b = c % 2
                k.dma("pool", w[b][:, :, 0:128], wq[:, :, c * 128:(c + 1) * 128], wS[b][0], writes=[wT[b][0]])
                k.dma("pool", w[b][:, :, 128:192], wq[:, :, D + kv * 64:D + (kv + 1) * 64], wS[b][1], writes=[wT[b][1]])
                k.dma("pool", w[b][:, :, 192:256], wq[:, :, D + kv * 64:D + (kv + 1) * 64], wS[b][2], writes=[wT[b][2]])
                k.dma("pool", w[b][:, :, 256:320], wq[:, :, D + 256 + kv * 64:D + 256 + (kv + 1) * 64], wS[b][3], writes=[wT[b][3]])
            loadw(0)
            sidx = 0
            for c in range(8):
                if c + 1 < 8:
                    loadw(c + 1)
                wc, wcT = w[c % 2], wT[c % 2]
                self.proj_fm(H, HT, wc, wcT[0], 0, qT, qT_, pq, pqT, rope)
                HT2 = HT + [wcT[2]]
                self.proj_fm(H, HT2, wc, wcT[1], 128, kT, kT_, pq, pqT, rope)
                for g0 in range(0, 34, 8):
                    g1 = min(34, g0 + 8)
                    p = self._pq
                    self._pq ^= 1
                    fns = []
                    for tt in range(g0, g1):
                        for kc in range(8):
                            fns.append(MM(pq[p][:, (tt - g0) * 64:(tt - g0 + 1) * 64], H[:, kc, tt * 128:(tt + 1) * 128],
                                          wc[:, kc, 256:320], start=(kc == 0), stop=(kc == 7)))
                    k.op("pe", fns, reads=[wcT[3]] + HT, writes=[pqT[p]])
                    k.op("act", ACT(V[:, g0:g1, 0:64], pq[p][:, 0:(g1 - g0) * 64].rearrange("p (a b) -> p a b", b=64), AF.Copy),
                         reads=[pqT[p]], writes=[VT])
                OT, OTT = OTc[c % 2], OTcT[c % 2]
                tasks = [(n, jh) for n in range(32) for jh in range(2)]

                def stageA(i):
                    n, jh = tasks[i]
                    pl = slice(jh * 64, (jh + 1) * 64)
                    sp_ = i % 2
                    tiles = [t for t in (n - 1, n, n + 1) if 0 <= t < 32]
                    nl = len(tiles)
                    k.op("pe", [MM(pq[sp_][:, a * 128:(a + 1) * 128], kT[pl, kt * 128:(kt + 1) * 128], qT[pl, n * 128:(n + 1) * 128])
                                for a, kt in enumerate(tiles)], reads=[kT_, qT_], writes=[pqT[sp_]])
                    k.op("pe", [MM(pcx[sp_][:, tt * 128:(tt + 1) * 128], kT[pl, S + tt * 128:S + (tt + 1) * 128], qT[pl, n * 128:(n + 1) * 128])
                                for tt in range(2)], reads=[kT_, qT_], writes=[pcxT[sp_]])
                    k.op("act", ACT(Pl[sp_][:, 0:nl, :], pq[sp_][:, 0:nl * 128].rearrange("p (a b) -> p a b", b=128), AF.Exp, scale=0.125),
                         reads=[pqT[sp_]], writes=[PlT[sp_]])
                    k.op("act", ACT(Pcx[sp_][:, :, :], pcx[sp_][:, :].rearrange("p (a b) -> p a b", b=128), AF.Exp, scale=0.125),
                         reads=[pcxT[sp_]], writes=[PcxT[sp_]])
                    for a, kt in enumerate(tiles):
                        if kt == n - 1:
                            k.op("pool", TT(Pl[sp_][:, a, :], Pl[sp_][:, a, :], mprev, ALU.mult), reads=[PlT[sp_], self.cstT], writes=[PlT[sp_]])
                        elif kt == n + 1:
                            k.op("pool", TT(Pl[sp_][:, a, :], Pl[sp_][:, a, :], mnext, ALU.mult), reads=[PlT[sp_], self.cstT], writes=[PlT[sp_]])

                def stageB(i):
                    n, jh = tasks[i]
                    sp_, ob = i % 2, n % 2
                    hq = 2 * c + jh
                    tiles = [t for t in (n - 1, n, n + 1) if 0 <= t < 32]
                    fns = [MM(po[ob][:, jh, 0:65], Pl[sp_][:, a, :], V[:, kt, 0:65], start=(a == 0), stop=False) for a, kt in enumerate(tiles)]
                    fns += [MM(po[ob][:, jh, 0:65], Pcx[sp_][:, tt, :], V[:, 32 + tt, 0:65], start=False, stop=(tt == 1)) for tt in range(2)]
                    k.op("pe", fns, reads=[PlT[sp_], PcxT[sp_], VT], writes=[poT[ob]])
                    k.op("dve", TS(sc[:, 2 * jh:2 * jh + 1], po[ob][:, jh, 64:65], self.small[:, 8 + hq:9 + hq], op0=ALU.add),
                         reads=[poT[ob], self.smallT], writes=[scT])
                    k.op("dve", RCP(sc[:, 2 * jh + 1:2 * jh + 2], sc[:, 2 * jh:2 * jh + 1]), reads=[scT], writes=[scT])
                    k.op("dve", TS(on[ob][:, jh * 64:(jh + 1) * 64], po[ob][:, jh, 0:64], sc[:, 2 * jh + 1:2 * jh + 2]), reads=[poT[ob], scT], writes=[onT[ob]])
                    if jh != 1:
                        return
                    k.op("pe", TR(ptr[:], on[ob][:], ident), reads=[onT[ob], self.cstT], writes=[ptrT])
                    k.op("dve", CP(OT[:, n * 128:(n + 1) * 128], ptr[:]), reads=[ptrT], writes=[OTT])

                stageA(0)
                for i in range(len(tasks)):
                    if i + 1 < len(tasks):
                        stageA(i + 1)
                    stageB(i)
                if need_ctx:
                    for jh in range(2):
                        pl = slice(jh * 64, (jh + 1) * 64)
                        hq = 2 * c + jh
                        self.ctx_dense((pq, pqT, pc, pcT, pr, prT, sc, scT), qT, kT, qT_, kT_, pl,
                                       (lambda tt: V[:, 32 + tt, 0:65]), VT, 64, self.small[:, 8 + hq:9 + hq],
                                       (lambda qs, jh=jh: oc[:, qs, jh * 64:(jh + 1) * 64]), ocT)
                    for qs in range(2):
                        k.op("pe", TR(ptr[:], oc[:, qs, :], ident), reads=[ocT, self.cstT], writes=[ptrT])
                        k.op("dve", CP(OT[:, S + qs * 128:S + (qs + 1) * 128], ptr[:]), reads=[ptrT], writes=[OTT])
                ntok = NT if need_ctx else S
                k.dma("sp", self.oT[c * 128:(c + 1) * 128, :ntok], OT[:, :ntok], OTcS[c % 2], reads=[OTT])


_CACHE = {}


def _get_prog(layers, dbg):
    key = (tuple(layers), dbg)
    if key not in _CACHE:
        _CACHE[key] = Prog(layers, dbg)
    return _CACHE[key]


def _in_maps(inp, xT_list, prog):
    cos, sin, cst = _const_tables()
    natab = _na_table(np.asarray(inp["b_rpb"][0], np.float32))
    L = prog.layers
    ja = sorted(prog.ja)
    shared = {}
    for n, sh in prog.wshapes.items():
        if sh == [1, 8, 8]:
            shared[n] = np.zeros(sh, np.float32)
        elif n in ("ada_w", "ffn_up", "ffn_down"):
            shared[n] = np.ascontiguousarray(np.asarray(inp[n], np.float32)[L])
        elif n in ("a_wqkv", "a_wo"):
            shared[n] = np.ascontiguousarray(np.asarray(inp[n], np.float32)[ja])
        else:
            shared[n] = np.ascontiguousarray(np.asarray(inp[n], np.float32))
    maps = []
    for b, xT in enumerate(xT_list):
        m = {"xT": xT, "vec": _pack_vec(inp, b), "cos": cos, "sin": sin, "cst": cst, "natab": natab}
        m.update(shared)
        maps.append(m)
    return maps


def kernel(**inp):
    inp = {n: np.asarray(v) for n, v in inp.items()}
    B = inp["x"].shape[0]
    xT_list = [np.ascontiguousarray(np.concatenate([inp["x"][b], inp["ctx"][b]], axis=0).T.astype(np.float32)) for b in range(B)]
    prog = _get_prog(range(DEPTH), False)
    res = run_bass_kernel_spmd(prog.nc, _in_maps(inp, xT_list, prog), core_ids=list(range(B)))
    out = np.stack([np.ascontiguousarray(res.results[b]["outT"].T) for b in range(B)], axis=0)
    return out.astype(np.float32)
```
